# Optimizing a Trainium2 kernel written in Bass

```python
import math
import jax, jax.numpy as jnp
from jax import lax
import numpy as np

D_MODEL = 1024
BATCH = 8
SEQ = 4096
DEPTH = 1

CHUNK = 64
N_META = 16
POOL_WINDOWS = (2, 4, 8, 16)
POOL_GROUPS = len(POOL_WINDOWS)
D_POOL = D_MODEL // 2
POOL_GW = D_POOL // POOL_GROUPS
N_HEADS = 8
QK_NOPE = 64
QK_ROPE = 32
V_DIM = 64
Q_LORA = 384
KV_LORA = 256
ROPE_THETA = 10000.0
Q_BLOCK = 128
ATTN_SCALE = (QK_NOPE + QK_ROPE) ** -0.5
N_BRANCH = 2
D_FF = 2816
CONV_W = 3
EPS = 1e-6
ALPHA = (2.0 * DEPTH) ** 0.25
BETA = (8.0 * DEPTH) ** -0.25
SPLIT_SIZES = (D_POOL, Q_LORA, KV_LORA, QK_ROPE, N_BRANCH * D_MODEL)
D_IN = sum(SPLIT_SIZES)
SPLIT_POINTS = tuple(int(v) for v in np.cumsum(SPLIT_SIZES)[:-1])

kernel_name = "hybrid_pool_mla_gated_convffn_deepnorm"


def layer_norm(x, g, b):
    xf = x.astype(jnp.float32)
    mu = jnp.mean(xf, axis=-1, keepdims=True)
    var = jnp.mean(jnp.square(xf - mu), axis=-1, keepdims=True)
    return ((xf - mu) * lax.rsqrt(var + EPS)).astype(x.dtype) * g + b


def rms_norm(x, g):
    xf = x.astype(jnp.float32)
    ms = jnp.mean(jnp.square(xf), axis=-1, keepdims=True)
    return (xf * lax.rsqrt(ms + EPS)).astype(x.dtype) * g


def rope_tables(L, dtype):
    pos = jnp.arange(L, dtype=jnp.float32)
    inv = ROPE_THETA ** (-jnp.arange(0, QK_ROPE, 2, dtype=jnp.float32) / QK_ROPE)
    ang = pos[:, None] * inv[None, :]
    return jnp.cos(ang).astype(dtype), jnp.sin(ang).astype(dtype)


def apply_rope(x, cos, sin):
    x1, x2 = jnp.split(x, 2, axis=-1)
    return jnp.concatenate([x1 * cos - x2 * sin, x2 * cos + x1 * sin], axis=-1)


def multiscale_pool(v, pool_w, pool_scale):
    B, L, _ = v.shape
    vg = v.reshape(B, L, POOL_GROUPS, POOL_GW)
    cs = jnp.cumsum(vg.astype(jnp.float32), axis=1)
    cs0 = jnp.concatenate([jnp.zeros_like(cs[:, :1]), cs], axis=1)
    t = jnp.arange(L)
    means = []
    for g, w in enumerate(POOL_WINDOWS):
        upper = cs0[:, 1:, g]
        lower = cs0[:, jnp.maximum(t + 1 - w, 0), g]
        cnt = jnp.minimum(t + 1, w).astype(jnp.float32)[None, :, None]
        means.append((upper - lower) / cnt)
    mean = jnp.stack(means, axis=2).astype(v.dtype)
    y = jnp.einsum('blgc,gcd->blgd', mean - vg, pool_w)
    return y.reshape(B, L, D_POOL) * pool_scale


def key_extent(q_last, L):
    c = (q_last - N_META) // CHUNK
    return min(L, N_META + CHUNK * (c + 1))


def mla_attention(c_q, c_kv, k_rope, q_norm_g, w_uq, kv_norm_g, w_uk, w_uv, cos, sin):
    B, L, _ = c_q.shape
    q = jnp.einsum('blr,rhd->blhd', rms_norm(c_q, q_norm_g), w_uq)
    q_nope, q_rope = q[..., :QK_NOPE], q[..., QK_NOPE:]
    q_rope = apply_rope(q_rope, cos[None, :, None, :], sin[None, :, None, :])
    ckv = rms_norm(c_kv, kv_norm_g)
    k_nope = jnp.einsum('blr,rhd->blhd', ckv, w_uk)
    v = jnp.einsum('blr,rhd->blhd', ckv, w_uv)
    k_rope = apply_rope(k_rope, cos[None], sin[None])
    cid = (jnp.arange(L) - N_META) // CHUNK
    neg = jnp.finfo(jnp.float32).min
    outs = []
    for s in range(0, L, Q_BLOCK):
        e = min(s + Q_BLOCK, L)
        ke = key_extent(e - 1, L)
        sc = (jnp.einsum('bqhd,bkhd->bhqk', q_nope[:, s:e], k_nope[:, :ke])
              + jnp.einsum('bqhr,bkr->bhqk', q_rope[:, s:e], k_rope[:, :ke]))
        sc = sc.astype(jnp.float32) * ATTN_SCALE
        mask = cid[s:e, None] >= cid[None, :ke]
        sc = jnp.where(mask[None, None], sc, neg)
        p = jax.nn.softmax(sc, axis=-1).astype(v.dtype)
        outs.append(jnp.einsum('bhqk,bkhd->bqhd', p, v[:, :ke]))
    o = jnp.concatenate(outs, axis=1)
    return o.reshape(B, L, N_HEADS * V_DIM)


def token_mixer(u, w_in, pool_w, pool_scale, p_pool, q_norm_g, w_uq, kv_norm_g, w_uk,
                w_uv, p_mla, b_gate, w_out, cos, sin):
    B, L, _ = u.shape
    z = u @ w_in
    v_pool, c_q, c_kv, k_rope, g_logit = jnp.split(z, SPLIT_POINTS, axis=-1)
    y_pool = multiscale_pool(v_pool, pool_w, pool_scale) @ p_pool
    y_mla = mla_attention(c_q, c_kv, k_rope, q_norm_g, w_uq, kv_norm_g, w_uk, w_uv,
                          cos, sin) @ p_mla
    g = jax.nn.sigmoid(g_logit + b_gate).reshape(B, L, N_BRANCH, D_MODEL)
    merged = g[:, :, 0] * y_pool + g[:, :, 1] * y_mla
    return merged @ w_out


def conv_ffn(h, w_up, conv_w, conv_b, w_down):
    L = h.shape[1]
    a = h @ w_up
    ap = jnp.pad(a, ((0, 0), (CONV_W - 1, 0), (0, 0)))
    c = ap[:, 0:L] * conv_w[0]
    for k in range(1, CONV_W):
        c = c + ap[:, k:k + L] * conv_w[k]
    c = c + conv_b
    gate, up = jnp.split(c, 2, axis=-1)
    return (jax.nn.silu(gate) * up) @ w_down


def setup_inputs(seed: int = 0) -> dict:
    key = jax.random.key(seed)
    ks = jax.random.split(key, 32)
    f = jnp.float32
    n = lambda k, shape, s: jax.random.normal(k, shape, f) * s
    D = D_MODEL
    return {
        "x": n(ks[0], (BATCH, SEQ, D), 1.0),
        "meta": n(ks[1], (N_META, D), 1.0),
        "ln_in_g": 1.0 + n(ks[2], (D,), 0.02),
        "ln_in_b": n(ks[3], (D,), 0.02),
        "w_in": n(ks[4], (DEPTH, D, D_IN), D ** -0.5),
        "pool_w": n(ks[5], (DEPTH, POOL_GROUPS, POOL_GW, POOL_GW), POOL_GW ** -0.5),
        "pool_scale": 1.0 + n(ks[6], (DEPTH, D_POOL), 0.1),
        "p_pool": n(ks[7], (DEPTH, D_POOL, D), BETA * D_POOL ** -0.5),
        "q_norm_g": 1.0 + n(ks[8], (DEPTH, Q_LORA), 0.02),
        "w_uq": n(ks[9], (DEPTH, Q_LORA, N_HEADS, QK_NOPE + QK_ROPE), Q_LORA ** -0.5),
        "kv_norm_g": 1.0 + n(ks[10], (DEPTH, KV_LORA), 0.02),
        "w_uk": n(ks[11], (DEPTH, KV_LORA, N_HEADS, QK_NOPE), KV_LORA ** -0.5),
        "w_uv": n(ks[12], (DEPTH, KV_LORA, N_HEADS, V_DIM), KV_LORA ** -0.5),
        "p_mla": n(ks[13], (DEPTH, N_HEADS * V_DIM, D), BETA * (N_HEADS * V_DIM) ** -0.5),
        "b_gate": n(ks[14], (DEPTH, N_BRANCH * D), 0.01),
        "w_out": n(ks[15], (DEPTH, D, D), BETA * D ** -0.5),
        "ln1_g": 1.0 + n(ks[16], (DEPTH, D), 0.02),
        "ln1_b": n(ks[17], (DEPTH, D), 0.02),
        "w_ffn_up": n(ks[18], (DEPTH, D, 2 * D_FF), D ** -0.5),
        "ffn_conv_w": n(ks[19], (DEPTH, CONV_W, 2 * D_FF), CONV_W ** -0.5),
        "ffn_conv_b": n(ks[20], (DEPTH, 2 * D_FF), 0.02),
        "w_ffn_down": n(ks[21], (DEPTH, D_FF, D), BETA * D_FF ** -0.5),
        "ln2_g": 1.0 + n(ks[22], (DEPTH, D), 0.02),
        "ln2_b": n(ks[23], (DEPTH, D), 0.02),
    }


def reference(x, meta, ln_in_g, ln_in_b, w_in, pool_w, pool_scale, p_pool, q_norm_g,
              w_uq, kv_norm_g, w_uk, w_uv, p_mla, b_gate, w_out, ln1_g, ln1_b,
              w_ffn_up, ffn_conv_w, ffn_conv_b, w_ffn_down, ln2_g, ln2_b):
    B = x.shape[0]
    h = jnp.concatenate(
        [jnp.broadcast_to(meta[None].astype(x.dtype), (B, N_META, D_MODEL)), x], axis=1)
    L = h.shape[1]
    h = layer_norm(h, ln_in_g, ln_in_b)
    cos, sin = rope_tables(L, h.dtype)
    for i in range(DEPTH):
        t = token_mixer(h, w_in[i], pool_w[i], pool_scale[i], p_pool[i], q_norm_g[i],
                        w_uq[i], kv_norm_g[i], w_uk[i], w_uv[i], p_mla[i], b_gate[i],
                        w_out[i], cos, sin)
        h = layer_norm(ALPHA * h + t, ln1_g[i], ln1_b[i])
        f = conv_ffn(h, w_ffn_up[i], ffn_conv_w[i], ffn_conv_b[i], w_ffn_down[i])
        h = layer_norm(ALPHA * h + f, ln2_g[i], ln2_b[i])
    return h[:, N_META:]
```

```python
import numpy as np
from contextlib import ExitStack
import concourse.bass as bass
import concourse.mybir as mybir
from concourse.bass_utils import run_bass_kernel_spmd

F32 = mybir.dt.float32
BF16 = mybir.dt.bfloat16
ALU = mybir.AluOpType
AF = mybir.ActivationFunctionType

D = 1024
NMETA = 16
T = 512
EPS = 1e-6
ALPHA = 2.0 ** 0.25
ATTN_SCALE = 96.0 ** -0.5
DFF = 2816
NFC = 22
USZ = 4096
NSLOT = 5

U_INA, U_INB, U_INC = 0, 1, 2
U_PP, U_UQ, U_UKV, U_PMLA = 3, 4, 5, 6
U_G0 = 7
U_WO = 11
U_UP = 13
U_DN = 24
NUNITS = 30

C_BG = 0
C_CW = 16
C_CB = C_CW + 132
C_PS = C_CB + 44
C_QG = C_PS + 4
C_KG = C_QG + 3
C_IC = C_KG + 2
C_EPS = C_IC + 64
C_LT = C_EPS + 1
NCONST = C_LT + 32


class Buf:
    __slots__ = ("name", "w", "r")

    def __init__(self, name):
        self.name = name
        self.w = {}
        self.r = {}


class Prog:
    ENG = ("pe", "act", "dve", "pool", "sp")

    def __init__(self):
        self.ops = {e: [] for e in self.ENG}
        self.cnt = {}
        self.seen = {e: {} for e in self.ENG}
        self.nwaits = 0
        self.nops = 0

    def _deps(self, eng, reads, writes):
        deps = {}

        def add(src, val, raw):
            if src == eng and eng == "pe":
                return
            if deps.get(src, 0) < val:
                deps[src] = val

        for b in reads:
            for src, val in b.w.items():
                add(src, val, True)
        for b in writes:
            for src, val in b.w.items():
                add(src, val, False)
            for src, val in b.r.items():
                add(src, val, False)
        waits = []
        seen = self.seen[eng]
        for src, val in deps.items():
            if seen.get(src, 0) < val:
                seen[src] = val
                waits.append((src, val))
        self.nwaits += len(waits)
        return waits

    def emit(self, eng, fn, reads=(), writes=(), signal=True):
        waits = self._deps(eng, reads, writes)
        for src, val in waits:
            if src in self.ENG and val > self.cnt.get(src, 0):
                raise RuntimeError(f"wait on unsignaled event {src} {val} > {self.cnt.get(src, 0)}")
        if signal:
            self.cnt[eng] = self.cnt.get(eng, 0) + 1
            v = self.cnt[eng]
        else:
            v = self.cnt.get(eng, 0) + 1
        for b in reads:
            if b.r.get(eng, 0) < v:
                b.r[eng] = v
        for b in writes:
            if b.w.get(eng, 0) < v:
                b.w[eng] = v
        self.ops[eng].append((waits, fn, eng if signal else None, 1))
        self.nops += 1
        return (eng, v)

    def dma(self, qeng, fn, sem, reads=(), writes=()):
        waits = self._deps(qeng, reads, writes)
        self.cnt[sem] = self.cnt.get(sem, 0) + 16
        v = self.cnt[sem]
        for b in reads:
            b.r[sem] = v
        for b in writes:
            b.w[sem] = v
        self.ops[qeng].append((waits, fn, sem, 16))
        self.nops += 1
        return (sem, v)

    def wait_all(self, eng, evs):
        waits = []
        for src, val in evs:
            if self.seen[eng].get(src, 0) < val:
                self.seen[eng][src] = val
                waits.append((src, val))
        self.ops[eng].append((waits, None, None, 0))

    def build(self, nc, stack):
        names = sorted(self.cnt.keys())
        sems = {n: stack.enter_context(nc.semaphore("s_" + n)) for n in names}
        block = stack.enter_context(nc.Block())
        emap = {"pe": block.tensor, "act": block.scalar, "dve": block.vector,
                "pool": block.gpsimd, "sp": block.sync}
        for e in self.ENG:
            ops = self.ops[e]

            def body(eng, ops=ops):
                for waits, fn, incsem, incval in ops:
                    for src, val in waits:
                        eng.wait_ge(sems[src], val)
                    if fn is None:
                        continue
                    ins = fn(eng)
                    if incsem is not None:
                        ins.then_inc(sems[incsem], incval)
            emap[e](body)


DBG = {"stage": 0}


def build_program(NT):
    nc = bass.Bass("TRN2", target_bir_lowering=False)
    LTOT = NMETA + NT * T
    x_d = nc.dram_tensor("x", [NT * T, D], F32, kind="ExternalInput").ap()
    meta_d = nc.dram_tensor("meta", [NMETA, D], F32, kind="ExternalInput").ap()
    lnbc_d = nc.dram_tensor("lnbc", [6, 128, D], F32, kind="ExternalInput").ap()
    consts_d = nc.dram_tensor("consts", [128, NCONST], F32, kind="ExternalInput").ap()
    ident_d = nc.dram_tensor("ident", [128, 128], F32, kind="ExternalInput").ap()
    rope_d = nc.dram_tensor("rope", [2, 32, LTOT], F32, kind="ExternalInput").ap()
    poolw_d = nc.dram_tensor("poolw", [128, 512], F32, kind="ExternalInput").ap()
    wts_d = nc.dram_tensor("wts", [NUNITS, 128, USZ], F32, kind="ExternalInput").ap()
    out_d = nc.dram_tensor("out", [NT * T, D], F32, kind="ExternalOutput").ap()
    kvk_d = nc.dram_tensor("kvk", [8, 128, NT * T], BF16).ap()
    kvv_d = nc.dram_tensor("kvv", [8, 128, NT * T], BF16).ap()

    P = Prog()
    st = ExitStack()
    with st:
        def sb(name, shape, dt):
            return st.enter_context(nc.sbuf_tensor("sb_" + name, shape, dt))

        lnbc = [sb(f"lnbc{i}", [128, D], F32) for i in range(6)]
        consts = sb("consts", [128, NCONST], F32)
        ident = sb("ident", [128, 128], F32)
        ones = sb("ones", [128, 128], F32)
        poolw = sb("poolw", [128, 512], BF16)
        kmeta = sb("kmeta", [128, 8 * 128], BF16)
        vmeta = sb("vmeta", [128, 1024], BF16)
        ahalo = sb("ahalo", [128, 2 * 2 * NFC * 2], F32)
        slots = [sb(f"slot{i}", [128, USZ], BF16) for i in range(NSLOT)]
        htm = [sb(f"htm{i}", [128, D], F32) for i in range(4)]
        xn = [sb(f"xn{i}", [128, D], F32) for i in range(4)]
        hT = sb("hT", [128, 8 * T], BF16)
        hTm = sb("hTm", [128, 8 * NMETA], BF16)
        arena = sb("arena", [128, 22 * T], BF16)
        arena2 = [sb(f"ar2_{i}", [128, 516], F32) for i in range(10)]
        vp = sb("vp", [128, 4 * 528], F32)
        ptmp = [sb(f"ptmp{i}", [128, 528], F32) for i in range(2)]
        pmin = sb("pmin", [128, 4 * T], BF16)
        pm = sb("pm", [128, 4 * T], BF16)
        cn = sb("cn", [128, 5 * T], BF16)
        ropet = sb("ropet", [32, 2 * T], F32)
        kcur = sb("kcur", [128, 8 * T], BF16)
        vcur = sb("vcur", [128, 4 * 1024], BF16)
        pt3 = sb("pt3", [128, T], BF16)
        rec = sb("rec", [64, T], F32)
        sg = [sb(f"sg{i}", [128, T], F32) for i in range(2)]
        stat = sb("stat", [128, 32], F32)
        psum = [st.enter_context(nc.psum_tensor(f"ps{i}", [128, 512], F32)) for i in range(8)]

        B_lnbc = Buf("lnbc"); B_consts = Buf("consts"); B_ident = Buf("ident"); B_ones = Buf("ones")
        B_poolw = Buf("poolw"); B_kmeta = Buf("kmeta"); B_vmeta = Buf("vmeta")
        B_ahalo = [Buf(f"ahalo{i}") for i in range(4 * NFC)]
        B_slot = [Buf(f"slot{i}") for i in range(NSLOT)]
        B_htm = [Buf(f"htm{i}") for i in range(4)]
        B_xn = [Buf(f"xn{i}") for i in range(4)]
        B_hT = [Buf(f"hT{i}") for i in range(8)]
        B_hTm = Buf("hTm")
        B_ar = [Buf(f"ar{i}") for i in range(22)]
        B_ar2 = [Buf(f"ar2_{i}") for i in range(10)]
        B_vp = [Buf(f"vp{i}") for i in range(4)]
        B_ptmp = [Buf("ptmp0"), Buf("ptmp1")]
        B_pmin = [Buf(f"pmin{i}") for i in range(4)]
        B_pm = [Buf(f"pm{i}") for i in range(4)]
        B_cn = [Buf(f"cn{i}") for i in range(5)]
        B_ropet = Buf("ropet")
        B_kcur = [Buf(f"kcur{i}") for i in range(8)]
        B_vcur = [Buf(f"vcur{i}") for i in range(4)]
        B_pt3 = Buf("pt3"); B_rec = Buf("rec")
        B_sg = [Buf(f"sg{i}") for i in range(2)]
        B_stat = Buf("stat")
        sqs = arena[:, 8 * T:18 * T].bitcast(F32)
        B_sqs = [[B_ar[8 + 2 * i], B_ar[9 + 2 * i]] for i in range(5)]
        B_ps = [Buf(f"ps{i}") for i in range(8)]
        B_kvk = [Buf(f"kvk{i}") for i in range(NT)]
        B_kvv = [Buf(f"kvv{i}") for i in range(NT)]

        cst = lambda c0, n=1: consts[:, c0:c0 + n]

        for i in range(6):
            P.dma("sp", lambda e, i=i: e.dma_start(out=lnbc[i][:], in_=lnbc_d[i]), "cl", writes=[B_lnbc])
        P.dma("sp", lambda e: e.dma_start(out=consts[:], in_=consts_d), "cc", writes=[B_consts])
        P.dma("sp", lambda e: e.dma_start(out=ident[:], in_=ident_d), "ci", writes=[B_ident])
        P.dma("pool", lambda e: e.dma_start(out=poolw[:], in_=poolw_d), "c1", writes=[B_poolw])
        P.emit("dve", lambda e: e.memset(ones[:], 1.0), writes=[B_ones])
        P.emit("dve", lambda e: e.memset(kmeta[:], 0.0), writes=[B_kmeta])
        P.emit("dve", lambda e: e.memset(vmeta[:], 0.0), writes=[B_vmeta])
        P.emit("dve", lambda e: e.memset(vp[:], 0.0), writes=B_vp)
        P.emit("dve", lambda e: e.memset(ahalo[:], 0.0), writes=B_ahalo)
        P.emit("dve", lambda e: e.memset(kcur[:], 0.0), writes=B_kcur)
        vcur_v = vcur[:].rearrange("p (h s c) -> p h s c", h=8, s=4)
        for s in range(4):
            P.emit("dve", lambda e, s=s: e.memset(vcur_v[:, :, s, 64:128], 1.0), writes=[B_vcur[s]])
        vmeta_v = vmeta[:].rearrange("p (h c) -> p h c", h=8)
        P.emit("dve", lambda e: e.memset(vmeta_v[0:NMETA, :, 64:128], 1.0), writes=[B_vmeta])

        class Stream:
            def __init__(self):
                self.units = []
                self.uslot = {}
                self.next_dma = 0
                self.head = 0
                self.free = list(range(NSLOT))

            def plan(self, lst):
                self.units.extend(lst)

            def _issue(self, k, slot):
                u = self.units[k]
                sl = slots[slot]
                sem = f"sl{slot}"
                if u[0] == "w":
                    _, idx, ncols = u
                    P.dma("pool", lambda e: e.dma_start(out=sl[:, 0:ncols], in_=wts_d[idx, :, 0:ncols]),
                          sem, writes=[B_slot[slot]])
                else:
                    _, h, jp0, n = u
                    rd = [B_kvk[jp] for jp in range(jp0, jp0 + n)] + [B_kvv[jp] for jp in range(jp0, jp0 + n)]
                    semk = f"sk{slot}"
                    P.dma("sp", lambda e: e.dma_start(out=sl[0:96, 0:n * T], in_=kvk_d[h, 0:96, jp0 * T:(jp0 + n) * T]),
                          semk, reads=rd, writes=[B_slot[slot]])
                    P.dma("sp", lambda e: e.dma_start(out=sl[:, 2048:2048 + n * T], in_=kvv_d[h, :, jp0 * T:(jp0 + n) * T]),
                          semk, reads=rd, writes=[B_slot[slot]])

            def pump(self):
                while self.free and self.next_dma < len(self.units):
                    slot = self.free.pop(0)
                    self.uslot[self.next_dma] = slot
                    self._issue(self.next_dma, slot)
                    self.next_dma += 1

            def acquire(self, *key):
                k = self.head
                assert tuple(self.units[k][:len(key)]) == tuple(key), (self.units[k], key)
                self.pump()
                assert k in self.uslot, "stream deadlock: no free slot"
                self.head += 1
                slot = self.uslot[k]
                return slots[slot], B_slot[slot], slot

            def release(self, slot):
                self.free.append(slot)
                self.pump()

        S = Stream()

        def kv_groups(j):
            g = []
            jp0 = 0
            while jp0 < j:
                n = min(4, j - jp0)
                g.append((jp0, n))
                jp0 += n
            return g

        def tile_units(j):
            u = [("w", U_INB, 4096), ("w", U_INC, 2048), ("w", U_INA, 4096),
                 ("w", U_UQ, 3 * 1152), ("w", U_UKV, 2048)]
            for h in range(8):
                for (jp0, n) in kv_groups(max(j, 0)):
                    u.append(("kv", h, jp0, n))
            u += [("w", U_PP, 4096), ("w", U_PMLA, 4096)]
            u += [("w", U_G0 + i, 4096) for i in range(4)]
            u += [("w", U_WO + i, 4096) for i in range(2)]
            if j >= 0:
                u += [("w", U_UP + i, 4096) for i in range(11)]
                u += [("w", U_DN + i, 4096) for i in range(6)]
            return u

        for j in range(-1, NT):
            S.plan(tile_units(j))

        bank_state = {"i": 0, "lo": 0, "hi": 8}

        def nbank():
            lo, hi = bank_state["lo"], bank_state["hi"]
            i = bank_state["i"]
            if i < lo or i >= hi:
                i = lo
            bank_state["i"] = i + 1 if i + 1 < hi else lo
            return i

        def mm(out, lhsT, rhs, start, stop, reads, writes, signal):
            P.emit("pe", lambda e: e.matmul(out, lhsT=lhsT, rhs=rhs, start=start, stop=stop),
                   reads=reads, writes=writes, signal=signal)

        evac_rr = {"i": 0}

        def copy_evac(out, in_, reads, writes, force=None):
            evac_rr["i"] ^= 1
            if force == "act" or (force is None and evac_rr["i"]):
                P.emit("act", lambda e: e.activation(out=out, in_=in_, func=AF.Copy), reads=reads, writes=writes)
            else:
                P.emit("dve", lambda e: e.tensor_copy(out=out, in_=in_), reads=reads, writes=writes)

        def ln_normalize(h, b, R):
            P.emit("dve", lambda e: e.bn_stats(out=stat[0:R, 0:6], in_=h[0:R, 0:512]), reads=[b], writes=[B_stat])
            P.emit("dve", lambda e: e.bn_stats(out=stat[0:R, 6:12], in_=h[0:R, 512:1024]), reads=[b], writes=[B_stat])
            P.emit("dve", lambda e: e.bn_aggr(out=stat[0:R, 12:14], in_=stat[0:R, 0:12]), reads=[B_stat], writes=[B_stat])
            P.emit("act", lambda e: e.activation(out=stat[0:R, 14:15], in_=stat[0:R, 13:14], func=AF.Ln,
                                                 bias=consts[0:R, C_EPS:C_EPS + 1], scale=1.0),
                   reads=[B_stat, B_consts], writes=[B_stat])
            P.emit("act", lambda e: e.activation(out=stat[0:R, 15:16], in_=stat[0:R, 14:15], func=AF.Exp, scale=-0.5),
                   reads=[B_stat], writes=[B_stat])
            P.emit("dve", lambda e: e.tensor_scalar(out=stat[0:R, 16:17], in0=stat[0:R, 12:13], scalar1=stat[0:R, 15:16],
                                                    scalar2=-1.0, op0=ALU.mult, op1=ALU.mult),
                   reads=[B_stat], writes=[B_stat])
            P.emit("act", lambda e: e.activation(out=h[0:R, :], in_=h[0:R, :], func=AF.Identity,
                                                 bias=stat[0:R, 16:17], scale=stat[0:R, 15:16]),
                   reads=[b, B_stat], writes=[b])

        def ln_affine(h, b, R, gi, aff):
            P.emit(aff, lambda e: e.tensor_tensor(out=h[0:R, :], in0=h[0:R, :], in1=lnbc[gi][0:R, :], op=ALU.mult),
                   reads=[b, B_lnbc], writes=[b])
            P.emit(aff, lambda e: e.tensor_tensor(out=h[0:R, :], in0=h[0:R, :], in1=lnbc[gi + 1][0:R, :], op=ALU.add),
                   reads=[b, B_lnbc], writes=[b])

        def layer_norm(h, b, R, gi, aff="dve"):
            ln_normalize(h, b, R)
            ln_affine(h, b, R, gi, aff)

        def transpose_to_hT(src, bsrc, NS, R, Tt, lt, to_meta=False):
            for c in range(8):
                bk = nbank()
                for s in range(NS):
                    P.emit("pe", lambda e, s=s, c=c, bk=bk: e.transpose(psum[bk][:, s * 128:s * 128 + R],
                                                                       src[s][0:R, c * 128:(c + 1) * 128], ident[0:R, 0:R]),
                           reads=[bsrc[s], B_ident], writes=[B_ps[bk]], signal=(s == NS - 1))
                gcol = consts[:, C_LT + lt * 8 + c:C_LT + lt * 8 + c + 1]
                bcol = consts[:, C_LT + (lt + 1) * 8 + c:C_LT + (lt + 1) * 8 + c + 1]
                if to_meta:
                    dsto, bdst = hTm[:, c * NMETA:(c + 1) * NMETA], B_hTm
                else:
                    dsto, bdst = hT[:, c * T:c * T + Tt], B_hT[c]
                if c % 2 == 0:
                    P.emit("act", lambda e, dsto=dsto, bk=bk, gcol=gcol, bcol=bcol: e.activation(
                        out=dsto, in_=psum[bk][:, 0:Tt], func=AF.Identity, bias=bcol, scale=gcol),
                        reads=[B_ps[bk], B_consts], writes=[bdst])
                else:
                    P.emit("dve", lambda e, dsto=dsto, bk=bk, gcol=gcol, bcol=bcol: e.tensor_scalar(
                        out=dsto, in0=psum[bk][:, 0:Tt], scalar1=gcol, scalar2=bcol, op0=ALU.mult, op1=ALU.add),
                        reads=[B_ps[bk], B_consts], writes=[bdst])

        out_events = []

        def dump_tm():
            for s in range(4):
                ev = P.dma("sp", lambda e, s=s: e.dma_start(out=out_d[s * 128:(s + 1) * 128, :], in_=htm[s][:, :]), f"out{s}", reads=[B_htm[s]])
                out_events.append(ev)

        def dump_fm(ap, bufs, f0):
            for tc in range(8):
                ev = P.dma("pool", lambda e, tc=tc: e.dma_start(out=out_d[tc * 64:(tc + 1) * 64, f0:f0 + 128].rearrange("t f -> f t"), in_=ap[:, tc * 64:(tc + 1) * 64], allow_slow_non_contiguous=True), "outd", reads=bufs)
                out_events.append(ev)
        def tp(j):
            is_meta = j < 0
            return dict(is_meta=is_meta, Tt=NMETA if is_meta else T, NS=1 if is_meta else 4,
                        R=NMETA if is_meta else 128, pos0=0 if is_meta else NMETA + j * T)

        ra, rb = arena2[8], arena2[9]

        def phase_load_ln_in(j):
            c = tp(j)
            Tt, NS, R, pos0 = c["Tt"], c["NS"], c["R"], c["pos0"]
            if c["is_meta"]:
                P.dma("sp", lambda e: e.dma_start(out=xn[0][0:NMETA, :], in_=meta_d), "xin0", writes=[B_xn[0]])
            else:
                for s in range(NS):
                    P.dma("sp", lambda e, s=s: e.dma_start(out=xn[s][:, :], in_=x_d[j * T + s * 128:j * T + (s + 1) * 128, :]),
                          f"xin{s}", writes=[B_xn[s]])
            P.dma("sp", lambda e: e.dma_start(out=ropet[:, 0:Tt], in_=rope_d[0, :, pos0:pos0 + Tt]), "rope", writes=[B_ropet])
            P.dma("sp", lambda e: e.dma_start(out=ropet[:, T:T + Tt], in_=rope_d[1, :, pos0:pos0 + Tt]), "rope", writes=[B_ropet])
            for s in range(NS):
                ln_normalize(xn[s], B_xn[s], R)

        def phase_front(j):
            c = tp(j)
            is_meta, Tt, NS, R = c["is_meta"], c["Tt"], c["NS"], c["R"]
            bank_state["lo"], bank_state["hi"] = 0, 8

            def hTc(kc):
                return hT[:, kc * T:kc * T + Tt]

            cc = ropet[:, 0:Tt]
            ss = ropet[:, T:T + Tt]

            def rms_part1(wq, bq, ncols, chunks, ar0, sq0):
                for ci in range(chunks):
                    bk = nbank()
                    for kc in range(8):
                        mm(psum[bk][:, 0:Tt], wq[:, kc * ncols + ci * 128:kc * ncols + (ci + 1) * 128], hTc(kc),
                           kc == 0, kc == 7, [bq, B_hT[kc]], [B_ps[bk]], kc == 7)
                    a = arena2[ar0 + ci]
                    P.emit("act", lambda e, a=a, bk=bk: e.activation(out=a[:, 0:Tt], in_=psum[bk][:, 0:Tt], func=AF.Copy),
                           reads=[B_ps[bk]], writes=[B_ar2[ar0 + ci]])
                    P.emit("act", lambda e, ci=ci, bk=bk: e.activation(out=sqs[:, (sq0 + ci) * T:(sq0 + ci) * T + Tt], in_=psum[bk][:, 0:Tt], func=AF.Square),
                           reads=[B_ps[bk]], writes=B_sqs[sq0 + ci])

            def rms_part2(chunks, gcol, nfeat, ar0, cn0, rstd_i, sq0):
                bk2 = nbank()
                for ci in range(chunks):
                    mm(psum[bk2][:, 0:Tt], ones[:, :], sqs[:, (sq0 + ci) * T:(sq0 + ci) * T + Tt], ci == 0, ci == chunks - 1,
                       [B_ones] + B_sqs[sq0 + ci], [B_ps[bk2]], ci == chunks - 1)
                rs = arena2[rstd_i]
                P.emit("act", lambda e: e.activation(out=rs[:, 0:Tt], in_=psum[bk2][:, 0:Tt], func=AF.Ln,
                                                     bias=consts[:, C_EPS:C_EPS + 1], scale=1.0 / nfeat),
                       reads=[B_ps[bk2], B_consts], writes=[B_ar2[rstd_i]])
                P.emit("act", lambda e: e.activation(out=rs[:, 0:Tt], in_=rs[:, 0:Tt], func=AF.Exp, scale=-0.5),
                       reads=[B_ar2[rstd_i]], writes=[B_ar2[rstd_i]])
                for ci in range(chunks):
                    a = arena2[ar0 + ci]
                    P.emit("dve", lambda e, a=a, ci=ci: e.scalar_tensor_tensor(
                        out=cn[:, (cn0 + ci) * T:(cn0 + ci) * T + Tt], in0=a[:, 0:Tt], scalar=consts[:, gcol + ci:gcol + ci + 1],
                        in1=rs[:, 0:Tt], op0=ALU.mult, op1=ALU.mult),
                        reads=[B_ar2[ar0 + ci], B_ar2[rstd_i], B_consts], writes=[B_cn[cn0 + ci]])

            wB, bwB, slB = S.acquire("w", U_INB)
            rms_part1(wB, bwB, 512, 3, 0, 0)
            bkr = nbank()
            for kc in range(8):
                mm(psum[bkr][:, 0:Tt], wB[:, kc * 512 + 384:kc * 512 + 512], hTc(kc), kc == 0, kc == 7,
                   [bwB, B_hT[kc]], [B_ps[bkr]], kc == 7)
            S.release(slB)
            P.emit("dve", lambda e: e.tensor_tensor(out=ra[0:32, 0:Tt], in0=psum[bkr][0:32, 0:Tt], in1=cc, op=ALU.mult),
                   reads=[B_ps[bkr], B_ropet], writes=[B_ar2[8]])
            P.emit("dve", lambda e: e.tensor_tensor(out=rb[0:32, 0:Tt], in0=psum[bkr][32:64, 0:Tt], in1=ss, op=ALU.mult),
                   reads=[B_ps[bkr], B_ropet], writes=[B_ar2[9]])
            P.emit("dve", lambda e: e.tensor_tensor(out=ra[0:32, 0:Tt], in0=ra[0:32, 0:Tt], in1=rb[0:32, 0:Tt], op=ALU.add),
                   reads=[B_ar2[8], B_ar2[9]], writes=[B_ar2[8]])
            for h in range(8):
                if is_meta:
                    dst, bd = kmeta[64:96, h * 128:h * 128 + NMETA], B_kmeta
                else:
                    dst, bd = kcur[64:96, h * T:(h + 1) * T], B_kcur[h]
                copy_evac(dst, ra[0:32, 0:Tt], [B_ar2[8]], [bd], force="act")
            wC, bwC, slC = S.acquire("w", U_INC)
            rms_part1(wC, bwC, 256, 2, 3, 3)
            S.release(slC)

            w, bw, sl = S.acquire("w", U_INA)
            for g in range(4):
                bk = nbank()
                for kc in range(8):
                    mm(psum[bk][:, 0:Tt], w[:, kc * 512 + g * 128:kc * 512 + (g + 1) * 128], hTc(kc),
                       kc == 0, kc == 7, [bw, B_hT[kc]], [B_ps[bk]], kc == 7)
                P.emit("act", lambda e, g=g, bk=bk: e.activation(out=vp[:, g * 528 + 16:g * 528 + 16 + Tt],
                                                               in_=psum[bk][:, 0:Tt], func=AF.Copy),
                       reads=[B_ps[bk]], writes=[B_vp[g]])
            S.release(sl)

            rms_part2(3, C_QG, 384.0, 0, 0, 6, 0)
            rms_part2(2, C_KG, 256.0, 3, 3, 7, 3)
            wq, bwq, slq = S.acquire("w", U_UQ)
            NQ = 1152
            swb = []
            bank_state["lo"], bank_state["hi"] = 0, 3
            for gq in range(3):
                bk = nbank()
                swb.append(bk)
                for kc in range(3):
                    mm(psum[bk][:, 0:Tt], wq[:, kc * NQ + 768 + gq * 128:kc * NQ + 768 + (gq + 1) * 128],
                       cn[:, kc * T:kc * T + Tt], kc == 0, kc == 2, [bwq, B_cn[kc]], [B_ps[bk]], kc == 2)
            bank_state["lo"], bank_state["hi"] = 3, 8
            for h in range(8):
                bk = nbank()
                assert bk not in swb
                for kc in range(3):
                    mm(psum[bk][0:96, 0:Tt], wq[:, kc * NQ + h * 96:kc * NQ + (h + 1) * 96],
                       cn[:, kc * T:kc * T + Tt], kc == 0, kc == 2, [bwq, B_cn[kc]], [B_ps[bk]], kc == 2)
                P.emit("act", lambda e, h=h, bk=bk: e.activation(out=arena[0:64, h * T:h * T + Tt], in_=psum[bk][0:64, 0:Tt], func=AF.Copy),
                       reads=[B_ps[bk]], writes=[B_ar[h]])
                sbk = swb[h // 3]
                i3 = h % 3
                P.emit("dve", lambda e, bk=bk: e.tensor_tensor(out=ra[0:32, 0:Tt], in0=psum[bk][64:96, 0:Tt], in1=cc, op=ALU.mult),
                       reads=[B_ps[bk], B_ropet], writes=[B_ar2[8]])
                P.emit("dve", lambda e, sbk=sbk, i3=i3: e.tensor_tensor(out=rb[0:32, 0:Tt], in0=psum[sbk][i3 * 32:(i3 + 1) * 32, 0:Tt], in1=ss, op=ALU.mult),
                       reads=[B_ps[sbk], B_ropet], writes=[B_ar2[9]])
                P.emit("dve", lambda e, h=h: e.tensor_tensor(out=arena[64:96, h * T:h * T + Tt], in0=ra[0:32, 0:Tt], in1=rb[0:32, 0:Tt], op=ALU.add),
                       reads=[B_ar2[8], B_ar2[9]], writes=[B_ar[h]])
            S.release(slq)
            bank_state["lo"], bank_state["hi"] = 0, 8
            wkv, bwkv, slkv = S.acquire("w", U_UKV)
            for hp in range(4):
                bk = nbank()
                for kc in range(2):
                    mm(psum[bk][:, 0:Tt], wkv[:, kc * 1024 + hp * 128:kc * 1024 + (hp + 1) * 128],
                       cn[:, (3 + kc) * T:(3 + kc) * T + Tt], kc == 0, kc == 1, [bwkv, B_cn[3 + kc]], [B_ps[bk]], kc == 1)
                for hh in range(2):
                    h = 2 * hp + hh
                    if is_meta:
                        dst, bd = kmeta[0:64, h * 128:h * 128 + NMETA], B_kmeta
                    else:
                        dst, bd = kcur[0:64, h * T:(h + 1) * T], B_kcur[h]
                    copy_evac(dst, psum[bk][hh * 64:(hh + 1) * 64, 0:Tt], [B_ps[bk]], [bd], force="act")
            for s in range(NS):
                bk = nbank()
                for kc in range(2):
                    mm(psum[bk][0:R, :], cn[:, (3 + kc) * T + s * 128:(3 + kc) * T + s * 128 + R],
                       wkv[:, kc * 1024 + 512:kc * 1024 + 1024], kc == 0, kc == 1, [bwkv, B_cn[3 + kc]], [B_ps[bk]], kc == 1)
                src = psum[bk][0:R, :].rearrange("p (h c) -> p h c", h=8)
                if is_meta:
                    dst, bd = vmeta_v[0:R, :, 0:64], B_vmeta
                else:
                    dst, bd = vcur_v[:, :, s, 0:64], B_vcur[s]
                copy_evac(dst, src, [B_ps[bk]], [bd], force="act")
            S.release(slkv)
            if not is_meta and j < NT - 1:
                P.dma("sp", lambda e: e.dma_start(out=kvk_d[:, 0:96, j * T:(j + 1) * T].rearrange("h p t -> p h t"),
                                                  in_=kcur[0:96, :].rearrange("p (h t) -> p h t", h=8)),
                      "kvstk", reads=B_kcur, writes=[B_kvk[j]])
                P.dma("sp", lambda e: e.dma_start(out=kvv_d[:, :, j * T:(j + 1) * T].rearrange("h p t -> p h t"),
                                                  in_=vcur[:, :].rearrange("p (h t) -> p h t", h=8)),
                      "kvstv", reads=B_vcur, writes=[B_kvv[j]])

            wins = (2, 4, 8, 16)
            for g in range(4):
                v = vp[:, g * 528:(g + 1) * 528]
                bv = B_vp[g]
                n = 16 + Tt
                src, bsrc = v, bv
                lo = 0
                step = 1
                ti = 0
                while step < wins[g]:
                    dst, bdst = ptmp[ti], B_ptmp[ti]
                    nlo = lo + step
                    P.emit("dve", lambda e, dst=dst, src=src, nlo=nlo, step=step, n=n: e.tensor_tensor(
                        out=dst[:, nlo:n], in0=src[:, nlo:n], in1=src[:, nlo - step:n - step], op=ALU.add),
                        reads=[bsrc], writes=[bdst])
                    src, bsrc, lo = dst, bdst, nlo
                    step *= 2
                    ti ^= 1
                if is_meta:
                    P.emit("dve", lambda e, src=src, g=g: e.tensor_tensor(
                        out=src[:, 16:16 + Tt], in0=src[:, 16:16 + Tt], in1=consts[:, C_IC + g * 16:C_IC + (g + 1) * 16], op=ALU.mult),
                        reads=[bsrc, B_consts], writes=[bsrc])
                    P.emit("dve", lambda e, src=src, v=v, g=g: e.tensor_tensor(
                        out=pmin[:, g * T:g * T + Tt], in0=src[:, 16:16 + Tt], in1=v[:, 16:16 + Tt], op=ALU.subtract),
                        reads=[bsrc, bv], writes=[B_pmin[g]])
                else:
                    P.emit("dve", lambda e, src=src, v=v, g=g: e.scalar_tensor_tensor(
                        out=pmin[:, g * T:g * T + Tt], in0=src[:, 16:16 + Tt], scalar=1.0 / wins[g], in1=v[:, 16:16 + Tt],
                        op0=ALU.mult, op1=ALU.subtract),
                        reads=[bsrc, bv], writes=[B_pmin[g]])
                P.emit("act", lambda e, v=v: e.activation(out=v[:, 0:16], in_=v[:, Tt:Tt + 16], func=AF.Copy), reads=[bv], writes=[bv])

        def phase_attn(j):
            c = tp(j)
            is_meta, Tt = c["is_meta"], c["Tt"]
            bank_state["lo"], bank_state["hi"] = 2, 8
            PT = [(arena[:, 20 * T:21 * T], B_ar[20]), (arena[:, 21 * T:22 * T], B_ar[21]), (pt3[:, :], B_pt3)]
            pti = {"i": 0}
            LA = 2
            pend = []

            def attn_item(h, K, bK, V, bV, n0, first, last, obk, rel):
                sbk = nbank()
                q = arena[0:96, h * T + n0:h * T + Tt]
                mm(psum[sbk][:, n0:Tt], K, q, True, True, [bK, B_ar[h]], [B_ps[sbk]], True)
                pt, bpt = PT[pti["i"]]
                pti["i"] = (pti["i"] + 1) % 3

                def tail():
                    P.emit("act", lambda e: e.activation(out=pt[:, n0:Tt], in_=psum[sbk][:, n0:Tt], func=AF.Exp, scale=ATTN_SCALE),
                           reads=[B_ps[sbk]], writes=[bpt])
                    if rel:
                        P.emit("act", lambda e: e.memzero(pt[64:128, n0:n0 + 64]), writes=[bpt])
                    mm(psum[obk][:, n0:Tt], V, pt[:, n0:Tt], first, last, [bV, bpt], [B_ps[obk]], True)
                    if last:
                        P.emit("dve", lambda e: e.reciprocal(out=rec[0:64, 0:Tt], in_=psum[obk][64:128, 0:Tt]),
                               reads=[B_ps[obk]], writes=[B_rec])
                        hp, hh = h // 2, h % 2
                        P.emit("dve", lambda e: e.tensor_tensor(out=arena[hh * 64:(hh + 1) * 64, (16 + hp) * T:(16 + hp) * T + Tt],
                                                                in0=psum[obk][0:64, 0:Tt], in1=rec[0:64, 0:Tt], op=ALU.mult),
                               reads=[B_ps[obk], B_rec], writes=[B_ar[16 + hp]])
                pend.append(tail)
                if len(pend) > LA:
                    pend.pop(0)()

            for h in range(8):
                obk = h % 2
                items = [(kmeta[0:96, h * 128:(h + 1) * 128], B_kmeta, vmeta[:, h * 128:(h + 1) * 128], B_vmeta, 0, False, None)]
                for (jp0, n) in kv_groups(max(j, 0)):
                    ks, bks, slk = S.acquire("kv", h, jp0, n)
                    for bi in range(4 * n):
                        items.append((ks[0:96, bi * 128:(bi + 1) * 128], bks, ks[:, 2048 + bi * 128:2048 + (bi + 1) * 128], bks, 0,
                                      False, slk if bi == 4 * n - 1 else None))
                if not is_meta:
                    for kb in range(4):
                        items.append((kcur[0:96, h * T + kb * 128:h * T + (kb + 1) * 128], B_kcur[h],
                                      vcur[:, (h * 4 + kb) * 128:(h * 4 + kb + 1) * 128], B_vcur[kb], kb * 128, True, None))
                for ii, (K, bK, V, bV, n0, rel, relslot) in enumerate(items):
                    attn_item(h, K, bK, V, bV, n0, ii == 0, ii == len(items) - 1, obk, rel)
                    if relslot is not None:
                        while pend:
                            pend.pop(0)()
                        S.release(relslot)
            while pend:
                pend.pop(0)()
            bank_state["lo"], bank_state["hi"] = 0, 8

        def phase_merge_wout_ln1(j):
            c = tp(j)
            Tt, NS, R = c["Tt"], c["NS"], c["R"]

            def hTc(kc):
                return hT[:, kc * T:kc * T + Tt]

            for g in range(4):
                bk = nbank()
                mm(psum[bk][:, 0:Tt], poolw[:, g * 128:(g + 1) * 128], pmin[:, g * T:g * T + Tt], True, True,
                   [B_poolw, B_pmin[g]], [B_ps[bk]], True)
                P.emit("act", lambda e, g=g, bk=bk: e.activation(out=pm[:, g * T:g * T + Tt], in_=psum[bk][:, 0:Tt], func=AF.Identity,
                                                               scale=consts[:, C_PS + g:C_PS + g + 1]),
                       reads=[B_ps[bk], B_consts], writes=[B_pm[g]])
            wpp, bwpp, slpp = S.acquire("w", U_PP)
            wpm, bwpm, slpm = S.acquire("w", U_PMLA)
            for u in range(4):
                wg, bwg, slg = S.acquire("w", U_G0 + u)
                for mi in range(2):
                    m = 2 * u + mi
                    bky = nbank()
                    for g in range(4):
                        mm(psum[bky][:, 0:Tt], wpp[:, g * 1024 + m * 128:g * 1024 + (m + 1) * 128], pm[:, g * T:g * T + Tt],
                           g == 0, g == 3, [bwpp, B_pm[g]], [B_ps[bky]], g == 3)
                    bkg0 = nbank()
                    for kc in range(8):
                        mm(psum[bkg0][:, 0:Tt], wg[:, kc * 512 + mi * 256:kc * 512 + mi * 256 + 128], hTc(kc),
                           kc == 0, kc == 7, [bwg, B_hT[kc]], [B_ps[bkg0]], kc == 7)
                    bkm = nbank()
                    for hp in range(4):
                        mm(psum[bkm][:, 0:Tt], wpm[:, hp * 1024 + m * 128:hp * 1024 + (m + 1) * 128],
                           arena[:, (16 + hp) * T:(16 + hp) * T + Tt], hp == 0, hp == 3, [bwpm, B_ar[16 + hp]], [B_ps[bkm]], hp == 3)
                    bkg1 = nbank()
                    for kc in range(8):
                        mm(psum[bkg1][:, 0:Tt], wg[:, kc * 512 + mi * 256 + 128:kc * 512 + mi * 256 + 256], hTc(kc),
                           kc == 0, kc == 7, [bwg, B_hT[kc]], [B_ps[bkg1]], kc == 7)
                    P.emit("act", lambda e, m=m, bkg0=bkg0: e.activation(out=sg[0][:, 0:Tt], in_=psum[bkg0][:, 0:Tt], func=AF.Sigmoid,
                                                                       bias=consts[:, C_BG + m:C_BG + m + 1], scale=1.0),
                           reads=[B_ps[bkg0], B_consts], writes=[B_sg[0]])
                    P.emit("act", lambda e, m=m, bkg1=bkg1: e.activation(out=sg[1][:, 0:Tt], in_=psum[bkg1][:, 0:Tt], func=AF.Sigmoid,
                                                                       bias=consts[:, C_BG + 8 + m:C_BG + 8 + m + 1], scale=1.0),
                           reads=[B_ps[bkg1], B_consts], writes=[B_sg[1]])
                    P.emit("dve", lambda e, bky=bky: e.tensor_tensor(out=sg[0][:, 0:Tt], in0=sg[0][:, 0:Tt], in1=psum[bky][:, 0:Tt], op=ALU.mult),
                           reads=[B_sg[0], B_ps[bky]], writes=[B_sg[0]])
                    P.emit("dve", lambda e, bkm=bkm: e.tensor_tensor(out=sg[1][:, 0:Tt], in0=sg[1][:, 0:Tt], in1=psum[bkm][:, 0:Tt], op=ALU.mult),
                           reads=[B_sg[1], B_ps[bkm]], writes=[B_sg[1]])
                    P.emit("dve", lambda e, m=m: e.tensor_tensor(out=arena[:, (8 + m) * T:(8 + m) * T + Tt], in0=sg[0][:, 0:Tt], in1=sg[1][:, 0:Tt], op=ALU.add),
                           reads=[B_sg[0], B_sg[1]], writes=[B_ar[8 + m]])
                S.release(slg)
            S.release(slpp)
            S.release(slpm)
            wo0, bwo0, slo0 = S.acquire("w", U_WO)
            wo1, bwo1, slo1 = S.acquire("w", U_WO + 1)
            for s in range(NS):
                for half, (wo, bwo) in enumerate(((wo0, bwo0), (wo1, bwo1))):
                    bk = nbank()
                    for kc in range(8):
                        mm(psum[bk][0:R, :], arena[:, (8 + kc) * T + s * 128:(8 + kc) * T + s * 128 + R], wo[:, kc * 512:(kc + 1) * 512],
                           kc == 0, kc == 7, [bwo, B_ar[8 + kc]], [B_ps[bk]], kc == 7)
                    P.emit("dve", lambda e, s=s, half=half, bk=bk: e.scalar_tensor_tensor(
                        out=htm[s][0:R, half * 512:(half + 1) * 512], in0=xn[s][0:R, half * 512:(half + 1) * 512], scalar=ALPHA,
                        in1=psum[bk][0:R, :], op0=ALU.mult, op1=ALU.add),
                        reads=[B_xn[s], B_ps[bk]], writes=[B_htm[s]])
                ln_normalize(htm[s], B_htm[s], R)
            S.release(slo0)
            S.release(slo1)

        def phase_ffn_up(j):
            c = tp(j)
            is_meta, Tt = c["is_meta"], c["Tt"]
            rpar = j % 2 if j >= 0 else 0
            wpar = (j + 1) % 2

            def hTc(kc):
                return hT[:, kc * T:kc * T + Tt]

            dbl = 0
            for u in range(11):
                wu, bwu, slu = S.acquire("w", U_UP + u)
                for mi in range(2):
                    jc = 2 * u + mi
                    ab = [arena2[0 + dbl], arena2[2 + dbl]]
                    bab = [B_ar2[0 + dbl], B_ar2[2 + dbl]]
                    cb_ = [arena2[4 + dbl], arena2[6 + dbl]]
                    bcb = [B_ar2[4 + dbl], B_ar2[6 + dbl]]
                    sgb, bsgb = arena2[8 + dbl], B_ar2[8 + dbl]
                    for br in range(2):
                        ch = br * NFC + jc
                        col0 = br * 256 + mi * 128
                        hold = ahalo[:, (rpar * 44 + ch) * 2:(rpar * 44 + ch) * 2 + 2]
                        hnew = ahalo[:, (wpar * 44 + ch) * 2:(wpar * 44 + ch) * 2 + 2]
                        if j == 0:
                            bkm = nbank()
                            for kc in range(8):
                                mm(psum[bkm][:, 0:NMETA], wu[:, kc * 512 + col0:kc * 512 + col0 + 128], hTm[:, kc * NMETA:(kc + 1) * NMETA],
                                   kc == 0, kc == 7, [bwu, B_hTm], [B_ps[bkm]], kc == 7)
                            P.emit("act", lambda e, hold=hold, bkm=bkm: e.activation(out=hold, in_=psum[bkm][:, NMETA - 2:NMETA], func=AF.Copy),
                                   reads=[B_ps[bkm]], writes=[B_ahalo[rpar * 44 + ch]])
                        bk = nbank()
                        for kc in range(8):
                            mm(psum[bk][:, 0:Tt], wu[:, kc * 512 + col0:kc * 512 + col0 + 128], hTc(kc),
                               kc == 0, kc == 7, [bwu, B_hT[kc]], [B_ps[bk]], kc == 7)
                        a, ba = ab[br], bab[br]
                        cc_, bc = cb_[br], bcb[br]
                        P.emit("act", lambda e, hnew=hnew, bk=bk: e.activation(out=hnew, in_=psum[bk][:, Tt - 2:Tt], func=AF.Copy),
                               reads=[B_ps[bk]], writes=[B_ahalo[wpar * 44 + ch]])
                        if is_meta:
                            continue
                        w0 = consts[:, C_CW + ch:C_CW + ch + 1]
                        w1 = consts[:, C_CW + 44 + ch:C_CW + 44 + ch + 1]
                        w2 = consts[:, C_CW + 88 + ch:C_CW + 88 + ch + 1]
                        if br == 0:
                            P.emit("act", lambda e, a=a, hold=hold: e.activation(out=a[:, 0:2], in_=hold, func=AF.Copy),
                                   reads=[B_ahalo[rpar * 44 + ch]], writes=[ba])
                            P.emit("act", lambda e, a=a, bk=bk: e.activation(out=a[:, 2:2 + Tt], in_=psum[bk][:, 0:Tt], func=AF.Copy),
                                   reads=[B_ps[bk]], writes=[ba])
                        P.emit("act", lambda e, cc_=cc_, bk=bk, ch=ch, w2=w2: e.activation(
                            out=cc_[:, 0:Tt], in_=psum[bk][:, 0:Tt], func=AF.Identity,
                            bias=consts[:, C_CB + ch:C_CB + ch + 1], scale=w2),
                            reads=[B_ps[bk], B_consts], writes=[bc])
                        if br == 0:
                            P.emit("dve", lambda e, a=a, cc_=cc_, w1=w1: e.scalar_tensor_tensor(
                                out=cc_[:, 0:Tt], in0=a[:, 1:1 + Tt], scalar=w1, in1=cc_[:, 0:Tt],
                                op0=ALU.mult, op1=ALU.add), reads=[ba, bc, B_consts], writes=[bc])
                            P.emit("dve", lambda e, a=a, cc_=cc_, w0=w0: e.scalar_tensor_tensor(
                                out=cc_[:, 0:Tt], in0=a[:, 0:Tt], scalar=w0, in1=cc_[:, 0:Tt],
                                op0=ALU.mult, op1=ALU.add), reads=[ba, bc, B_consts], writes=[bc])
                        else:
                            P.emit("dve", lambda e, cc_=cc_, bk=bk, w1=w1: e.scalar_tensor_tensor(
                                out=cc_[:, 1:Tt], in0=psum[bk][:, 0:Tt - 1], scalar=w1, in1=cc_[:, 1:Tt],
                                op0=ALU.mult, op1=ALU.add), reads=[B_ps[bk], bc, B_consts], writes=[bc])
                            P.emit("dve", lambda e, cc_=cc_, bk=bk, w0=w0: e.scalar_tensor_tensor(
                                out=cc_[:, 2:Tt], in0=psum[bk][:, 0:Tt - 2], scalar=w0, in1=cc_[:, 2:Tt],
                                op0=ALU.mult, op1=ALU.add), reads=[B_ps[bk], bc, B_consts], writes=[bc])
                            P.emit("dve", lambda e, cc_=cc_, hold=hold, w0=w0: e.scalar_tensor_tensor(
                                out=cc_[:, 0:2], in0=hold, scalar=w0, in1=cc_[:, 0:2],
                                op0=ALU.mult, op1=ALU.add), reads=[B_ahalo[rpar * 44 + ch], bc, B_consts], writes=[bc])
                            P.emit("dve", lambda e, cc_=cc_, hold=hold, w1=w1: e.scalar_tensor_tensor(
                                out=cc_[:, 0:1], in0=hold[:, 1:2], scalar=w1, in1=cc_[:, 0:1],
                                op0=ALU.mult, op1=ALU.add), reads=[B_ahalo[rpar * 44 + ch], bc, B_consts], writes=[bc])
                    if not is_meta:
                        P.emit("act", lambda e, c0=cb_[0], sgb=sgb: e.activation(out=sgb[:, 0:Tt], in_=c0[:, 0:Tt], func=AF.Silu),
                               reads=[bcb[0]], writes=[bsgb])
                        P.emit("dve", lambda e, c1=cb_[1], jc=jc, sgb=sgb: e.tensor_tensor(out=arena[:, jc * T:(jc + 1) * T], in0=sgb[:, 0:Tt], in1=c1[:, 0:Tt], op=ALU.mult),
                               reads=[bsgb, bcb[1]], writes=[B_ar[jc]])
                    dbl ^= 1
                S.release(slu)

        def phase_wdown(j):
            for u in range(6):
                wd, bwd, sld = S.acquire("w", U_DN + u)
                for ki in range(4):
                    kc = 4 * u + ki
                    if kc >= NFC:
                        break
                    for s in range(4):
                        for half in range(2):
                            bk = s * 2 + half
                            mm(psum[bk][:, :], arena[:, kc * T + s * 128:kc * T + (s + 1) * 128], wd[:, ki * 1024 + half * 512:ki * 1024 + (half + 1) * 512],
                               kc == 0, kc == NFC - 1, [bwd, B_ar[kc]], [B_ps[bk]],
                               kc == NFC - 1 or (ki == 3 and bk == 7))
                S.release(sld)
            for s in range(4):
                for half in range(2):
                    bk = s * 2 + half
                    P.emit("dve", lambda e, s=s, half=half, bk=bk: e.scalar_tensor_tensor(
                        out=htm[s][:, half * 512:(half + 1) * 512], in0=htm[s][:, half * 512:(half + 1) * 512], scalar=ALPHA,
                        in1=psum[bk][:, :], op0=ALU.mult, op1=ALU.add),
                        reads=[B_htm[s], B_ps[bk]], writes=[B_htm[s]])

        def phase_ln2_out(j):
            for s in range(4):
                layer_norm(htm[s], B_htm[s], 128, 4, "pool")
                ev = P.dma("sp", lambda e, s=s: e.dma_start(out=out_d[j * T + s * 128:j * T + (s + 1) * 128, :], in_=htm[s][:, :]),
                           f"out{s}", reads=[B_htm[s]])
                out_events.append(ev)

        def aff_all(buf, bbuf, NS, R, gi):
            for s in range(NS):
                ln_affine(buf[s], bbuf[s], R, gi, "pool")

        phase_load_ln_in(-1)
        transpose_to_hT(xn, B_xn, 1, NMETA, NMETA, 0)
        aff_all(xn, B_xn, 1, NMETA, 0)
        phase_front(-1)
        phase_attn(-1)
        phase_merge_wout_ln1(-1)
        transpose_to_hT(htm, B_htm, 1, NMETA, NMETA, 2, to_meta=True)
        phase_load_ln_in(0)
        transpose_to_hT(xn, B_xn, 4, 128, T, 0)
        aff_all(xn, B_xn, 4, 128, 0)
        for j in range(NT):
            phase_front(j)
            if j > 0:
                phase_ln2_out(j - 1)
            phase_attn(j)
            phase_merge_wout_ln1(j)
            transpose_to_hT(htm, B_htm, 4, 128, T, 2)
            if j + 1 < NT:
                phase_load_ln_in(j + 1)
            phase_ffn_up(j)
            aff_all(htm, B_htm, 4, 128, 2)
            if j + 1 < NT:
                transpose_to_hT(xn, B_xn, 4, 128, T, 0)
                aff_all(xn, B_xn, 4, 128, 0)
            phase_wdown(j)
        phase_ln2_out(NT - 1)
        last = {}
        for src, val in out_events:
            last[src] = max(last.get(src, 0), val)
        P.wait_all("sp", list(last.items()))
        P.build(nc, st)
    print(f"[kernel] NT={NT} ops={P.nops} waits={P.nwaits} per-engine={ {e: len(P.ops[e]) for e in P.ENG} }", flush=True)
    return nc


def _unit(arr2d, kc, ncols_pad=None):
    n = arr2d.shape[1]
    a = arr2d.reshape(kc, 128, n).transpose(1, 0, 2).reshape(128, kc * n)
    out = np.zeros((128, USZ), np.float32)
    out[:, :kc * n] = a
    return out


def prep_shared(inp, LTOT):
    f = np.float32
    w_in = np.asarray(inp["w_in"][0], f)
    units = np.zeros((NUNITS, 128, USZ), f)
    units[U_INA] = _unit(w_in[:, 0:512], 8)
    B = np.zeros((1024, 512), f)
    B[:, 0:384] = w_in[:, 512:896]
    B[:, 384:416] = w_in[:, 1152:1184]
    B[:, 416:432] = w_in[:, 1168:1184]
    B[:, 432:448] = w_in[:, 1152:1168]
    units[U_INB] = _unit(B, 8)
    units[U_INC] = _unit(w_in[:, 896:1152], 8)
    units[U_PP] = _unit(np.asarray(inp["p_pool"][0], f), 4)
    wuq = np.asarray(inp["w_uq"][0], f)
    Q = np.zeros((384, 1152), f)
    Q[:, 0:768] = wuq.reshape(384, 768)
    for h in range(8):
        g, i = h // 3, h % 3
        c0 = 768 + g * 128 + i * 32
        Q[:, c0:c0 + 16] = wuq[:, h, 80:96]
        Q[:, c0 + 16:c0 + 32] = wuq[:, h, 64:80]
    units[U_UQ] = _unit(Q, 3)
    KV = np.concatenate([np.asarray(inp["w_uk"][0], f).reshape(256, 512), np.asarray(inp["w_uv"][0], f).reshape(256, 512)], axis=1)
    units[U_UKV] = _unit(KV, 2)
    units[U_PMLA] = _unit(np.asarray(inp["p_mla"][0], f), 4)
    for u in range(4):
        G = np.zeros((1024, 512), f)
        for mi in range(2):
            m = 2 * u + mi
            G[:, mi * 256:mi * 256 + 128] = w_in[:, 1184 + m * 128:1184 + (m + 1) * 128]
            G[:, mi * 256 + 128:mi * 256 + 256] = w_in[:, 1184 + 1024 + m * 128:1184 + 1024 + (m + 1) * 128]
        units[U_G0 + u] = _unit(G, 8)
    w_out = np.asarray(inp["w_out"][0], f)
    for half in range(2):
        units[U_WO + half] = _unit(w_out[:, half * 512:(half + 1) * 512], 8)
    w_up = np.asarray(inp["w_ffn_up"][0], f)
    for u in range(11):
        Ub = np.zeros((1024, 512), f)
        for mi in range(2):
            jc = 2 * u + mi
            Ub[:, mi * 128:(mi + 1) * 128] = w_up[:, jc * 128:(jc + 1) * 128]
            Ub[:, 256 + mi * 128:256 + (mi + 1) * 128] = w_up[:, DFF + jc * 128:DFF + (jc + 1) * 128]
        units[U_UP + u] = _unit(Ub, 8)
    w_dn = np.asarray(inp["w_ffn_down"][0], f)
    for u in range(6):
        k0, k1 = 4 * u, min(4 * u + 4, NFC)
        Dn = np.zeros((512, 1024), f)
        Dn[:(k1 - k0) * 128] = w_dn[k0 * 128:k1 * 128]
        units[U_DN + u] = _unit(Dn, 4)
    poolw = np.ascontiguousarray(np.asarray(inp["pool_w"][0], f).transpose(1, 0, 2).reshape(128, 512))
    lnbc = np.stack([np.broadcast_to(np.asarray(a, f).reshape(1, D), (128, D)) for a in
                     (inp["ln_in_g"], inp["ln_in_b"], inp["ln1_g"][0], inp["ln1_b"][0], inp["ln2_g"][0], inp["ln2_b"][0])]).copy()
    consts = np.zeros((128, NCONST), f)
    consts[:, C_BG:C_BG + 16] = np.asarray(inp["b_gate"][0], f).reshape(16, 128).T
    cw = np.asarray(inp["ffn_conv_w"][0], f)
    for k in range(3):
        consts[:, C_CW + k * 44:C_CW + (k + 1) * 44] = cw[k].reshape(44, 128).T
    consts[:, C_CB:C_CB + 44] = np.asarray(inp["ffn_conv_b"][0], f).reshape(44, 128).T
    consts[:, C_PS:C_PS + 4] = np.asarray(inp["pool_scale"][0], f).reshape(4, 128).T
    consts[:, C_QG:C_QG + 3] = np.asarray(inp["q_norm_g"][0], f).reshape(3, 128).T
    consts[:, C_KG:C_KG + 2] = np.asarray(inp["kv_norm_g"][0], f).reshape(2, 128).T
    for g, wv in enumerate((2, 4, 8, 16)):
        consts[:, C_IC + g * 16:C_IC + (g + 1) * 16] = (1.0 / np.minimum(np.arange(16) + 1, wv)).astype(f)[None, :]
    consts[:, C_EPS] = EPS
    for wi, a in enumerate((inp["ln_in_g"], inp["ln_in_b"], inp["ln1_g"][0], inp["ln1_b"][0])):
        consts[:, C_LT + wi * 8:C_LT + (wi + 1) * 8] = np.asarray(a, f).reshape(8, 128).T
    pos = np.arange(LTOT, dtype=f)
    inv = (f(10000.0) ** (-np.arange(0, 32, 2, dtype=f) / f(32))).astype(f)
    ang = (pos[None, :] * inv[:, None]).astype(f)
    cosv, sinv = np.cos(ang).astype(f), np.sin(ang).astype(f)
    rope = np.stack([np.concatenate([cosv, cosv], 0), np.concatenate([-sinv, sinv], 0)]).astype(f)
    return dict(meta=np.ascontiguousarray(np.asarray(inp["meta"], f)), lnbc=lnbc, consts=consts,
                ident=np.eye(128, dtype=f), rope=np.ascontiguousarray(rope), poolw=poolw, wts=units)


_CACHE = {}


def kernel(**inputs):
    x = np.asarray(inputs["x"], np.float32)
    Bn, Sq, _ = x.shape
    NT = Sq // T
    shared = prep_shared(inputs, NMETA + NT * T)
    if NT not in _CACHE:
        _CACHE[NT] = build_program(NT)
    nc = _CACHE[NT]
    in_maps = [dict(shared, x=np.ascontiguousarray(x[b])) for b in range(Bn)]
    res = run_bass_kernel_spmd(nc, in_maps, core_ids=list(range(Bn)))
    return np.stack([np.asarray(r["out"], np.float32) for r in res.results], axis=0)
```

```python
import numpy as np
from contextlib import ExitStack
import concourse.bass as bass
import concourse.mybir as mybir
from concourse.bass_utils import run_bass_kernel_spmd

F32 = mybir.dt.float32
BF16 = mybir.dt.bfloat16
ALU = mybir.AluOpType
AF = mybir.ActivationFunctionType

D = 1024
NMETA = 16
T = 512
EPS = 1e-6
ALPHA = 2.0 ** 0.25
ATTN_SCALE = 96.0 ** -0.5
DFF = 2816
NFC = 22
USZ = 4096
NSLOT = 5

U_INA, U_INB, U_INC = 0, 1, 2
U_PP, U_UQ, U_UKV, U_PMLA = 3, 4, 5, 6
U_G0 = 7
U_WO = 11
U_UP = 13
U_DN = 24
NUNITS = 30

C_BG = 0
C_CW = 16
C_CB = C_CW + 132
C_PS = C_CB + 44
C_QG = C_PS + 4
C_KG = C_QG + 3
C_IC = C_KG + 2
C_EPS = C_IC + 64
C_LT = C_EPS + 1
NCONST = C_LT + 32


class Buf:
    __slots__ = ("name", "w", "r")

    def __init__(self, name):
        self.name = name
        self.w = {}
        self.r = {}


class Prog:
    ENG = ("pe", "act", "dve", "pool", "sp")

    def __init__(self):
        self.ops = {e: [] for e in self.ENG}
        self.cnt = {}
        self.seen = {e: {} for e in self.ENG}
        self.nwaits = 0
        self.nops = 0

    def _deps(self, eng, reads, writes):
        deps = {}

        def add(src, val, raw):
            if src == eng and eng == "pe":
                return
            if deps.get(src, 0) < val:
                deps[src] = val

        for b in reads:
            for src, val in b.w.items():
                add(src, val, True)
        for b in writes:
            for src, val in b.w.items():
                add(src, val, False)
            for src, val in b.r.items():
                add(src, val, False)
        waits = []
        seen = self.seen[eng]
        for src, val in deps.items():
            if seen.get(src, 0) < val:
                seen[src] = val
                waits.append((src, val))
        self.nwaits += len(waits)
        return waits

    def emit(self, eng, fn, reads=(), writes=(), signal=True):
        waits = self._deps(eng, reads, writes)
        for src, val in waits:
            if src in self.ENG and val > self.cnt.get(src, 0):
                raise RuntimeError(f"wait on unsignaled event {src} {val} > {self.cnt.get(src, 0)}")
        if signal:
            self.cnt[eng] = self.cnt.get(eng, 0) + 1
            v = self.cnt[eng]
        else:
            v = self.cnt.get(eng, 0) + 1
        for b in reads:
            if b.r.get(eng, 0) < v:
                b.r[eng] = v
        for b in writes:
            if b.w.get(eng, 0) < v:
                b.w[eng] = v
        self.ops[eng].append((waits, fn, eng if signal else None, 1))
        self.nops += 1
        return (eng, v)

    def dma(self, qeng, fn, sem, reads=(), writes=()):
        waits = self._deps(qeng, reads, writes)
        self.cnt[sem] = self.cnt.get(sem, 0) + 16
        v = self.cnt[sem]
        for b in reads:
            b.r[sem] = v
        for b in writes:
            b.w[sem] = v
        self.ops[qeng].append((waits, fn, sem, 16))
        self.nops += 1
        return (sem, v)

    def wait_all(self, eng, evs):
        waits = []
        for src, val in evs:
            if self.seen[eng].get(src, 0) < val:
                self.seen[eng][src] = val
                waits.append((src, val))
        self.ops[eng].append((waits, None, None, 0))

    def build(self, nc, stack):
        names = sorted(self.cnt.keys())
        sems = {n: stack.enter_context(nc.semaphore("s_" + n)) for n in names}
        block = stack.enter_context(nc.Block())
        emap = {"pe": block.tensor, "act": block.scalar, "dve": block.vector,
                "pool": block.gpsimd, "sp": block.sync}
        for e in self.ENG:
            ops = self.ops[e]

            def body(eng, ops=ops):
                for waits, fn, incsem, incval in ops:
                    for src, val in waits:
                        eng.wait_ge(sems[src], val)
                    if fn is None:
                        continue
                    ins = fn(eng)
                    if incsem is not None:
                        ins.then_inc(sems[incsem], incval)
            emap[e](body)


DBG = {"stage": 0}


def build_program(NT):
    nc = bass.Bass("TRN2", target_bir_lowering=False)
    LTOT = NMETA + NT * T
    x_d = nc.dram_tensor("x", [NT * T, D], F32, kind="ExternalInput").ap()
    meta_d = nc.dram_tensor("meta", [NMETA, D], F32, kind="ExternalInput").ap()
    lnbc_d = nc.dram_tensor("lnbc", [6, 128, D], F32, kind="ExternalInput").ap()
    consts_d = nc.dram_tensor("consts", [128, NCONST], F32, kind="ExternalInput").ap()
    ident_d = nc.dram_tensor("ident", [128, 128], F32, kind="ExternalInput").ap()
    rope_d = nc.dram_tensor("rope", [2, 32, LTOT], F32, kind="ExternalInput").ap()
    poolw_d = nc.dram_tensor("poolw", [128, 512], F32, kind="ExternalInput").ap()
    wts_d = nc.dram_tensor("wts", [NUNITS, 128, USZ], F32, kind="ExternalInput").ap()
    out_d = nc.dram_tensor("out", [NT * T, D], F32, kind="ExternalOutput").ap()
    kvk_d = nc.dram_tensor("kvk", [8, 128, NT * T], BF16).ap()
    kvv_d = nc.dram_tensor("kvv", [8, 128, NT * T], BF16).ap()

    P = Prog()
    st = ExitStack()
    with st:
        def sb(name, shape, dt):
            return st.enter_context(nc.sbuf_tensor("sb_" + name, shape, dt))

        lnbc = [sb(f"lnbc{i}", [128, D], F32) for i in range(6)]
        consts = sb("consts", [128, NCONST], F32)
        ident = sb("ident", [128, 128], F32)
        ones = sb("ones", [128, 128], F32)
        poolw = sb("poolw", [128, 512], BF16)
        kmeta = sb("kmeta", [128, 8 * 128], BF16)
        vmeta = sb("vmeta", [128, 1024], BF16)
        ahalo = sb("ahalo", [128, 2 * 2 * NFC * 2], F32)
        slots = [sb(f"slot{i}", [128, USZ], BF16) for i in range(NSLOT)]
        htm = [sb(f"htm{i}", [128, D], F32) for i in range(4)]
        xn = [sb(f"xn{i}", [128, D], F32) for i in range(4)]
        hT = sb("hT", [128, 8 * T], BF16)
        hTm = sb("hTm", [128, 8 * NMETA], BF16)
        arena = sb("arena", [128, 22 * T], BF16)
        arena2 = [sb(f"ar2_{i}", [128, 516], F32) for i in range(10)]
        vp = sb("vp", [128, 4 * 528], F32)
        ptmp = [sb(f"ptmp{i}", [128, 528], F32) for i in range(2)]
        pmin = sb("pmin", [128, 4 * T], BF16)
        pm = sb("pm", [128, 4 * T], BF16)
        cn = sb("cn", [128, 5 * T], BF16)
        ropet = sb("ropet", [32, 2 * T], F32)
        kcur = sb("kcur", [128, 8 * T], BF16)
        vcur = sb("vcur", [128, 4 * 1024], BF16)
        pt3 = sb("pt3", [128, T], BF16)
        rec = sb("rec", [64, T], F32)
        sg = [sb(f"sg{i}", [128, T], F32) for i in range(2)]
        stat = sb("stat", [128, 32], F32)
        psum = [st.enter_context(nc.psum_tensor(f"ps{i}", [128, 512], F32)) for i in range(8)]

        B_lnbc = Buf("lnbc"); B_consts = Buf("consts"); B_ident = Buf("ident"); B_ones = Buf("ones")
        B_poolw = Buf("poolw"); B_kmeta = Buf("kmeta"); B_vmeta = Buf("vmeta")
        B_ahalo = [Buf(f"ahalo{i}") for i in range(4 * NFC)]
        B_slot = [Buf(f"slot{i}") for i in range(NSLOT)]
        B_htm = [Buf(f"htm{i}") for i in range(4)]
        B_xn = [Buf(f"xn{i}") for i in range(4)]
        B_hT = [Buf(f"hT{i}") for i in range(8)]
        B_hTm = Buf("hTm")
        B_ar = [Buf(f"ar{i}") for i in range(22)]
        B_ar2 = [Buf(f"ar2_{i}") for i in range(10)]
        B_vp = [Buf(f"vp{i}") for i in range(4)]
        B_ptmp = [Buf("ptmp0"), Buf("ptmp1")]
        B_pmin = [Buf(f"pmin{i}") for i in range(4)]
        B_pm = [Buf(f"pm{i}") for i in range(4)]
        B_cn = [Buf(f"cn{i}") for i in range(5)]
        B_ropet = Buf("ropet")
        B_kcur = [Buf(f"kcur{i}") for i in range(8)]
        B_vcur = [Buf(f"vcur{i}") for i in range(4)]
        B_pt3 = Buf("pt3"); B_rec = Buf("rec")
        B_sg = [Buf(f"sg{i}") for i in range(2)]
        B_stat = Buf("stat")
        sqs = arena[:, 8 * T:18 * T].bitcast(F32)
        B_sqs = [[B_ar[8 + 2 * i], B_ar[9 + 2 * i]] for i in range(5)]
        B_ps = [Buf(f"ps{i}") for i in range(8)]
        B_kvk = [Buf(f"kvk{i}") for i in range(NT)]
        B_kvv = [Buf(f"kvv{i}") for i in range(NT)]

        cst = lambda c0, n=1: consts[:, c0:c0 + n]

        for i in range(6):
            P.dma("sp", lambda e, i=i: e.dma_start(out=lnbc[i][:], in_=lnbc_d[i]), "cl", writes=[B_lnbc])
        P.dma("sp", lambda e: e.dma_start(out=consts[:], in_=consts_d), "cc", writes=[B_consts])
        P.dma("sp", lambda e: e.dma_start(out=ident[:], in_=ident_d), "ci", writes=[B_ident])
        P.dma("pool", lambda e: e.dma_start(out=poolw[:], in_=poolw_d), "c1", writes=[B_poolw])
        P.emit("dve", lambda e: e.memset(ones[:], 1.0), writes=[B_ones])
        P.emit("dve", lambda e: e.memset(kmeta[:], 0.0), writes=[B_kmeta])
        P.emit("dve", lambda e: e.memset(vmeta[:], 0.0), writes=[B_vmeta])
        P.emit("dve", lambda e: e.memset(vp[:], 0.0), writes=B_vp)
        P.emit("dve", lambda e: e.memset(ahalo[:], 0.0), writes=B_ahalo)
        P.emit("dve", lambda e: e.memset(kcur[:], 0.0), writes=B_kcur)
        vcur_v = vcur[:].rearrange("p (h s c) -> p h s c", h=8, s=4)
        for s in range(4):
            P.emit("dve", lambda e, s=s: e.memset(vcur_v[:, :, s, 64:128], 1.0), writes=[B_vcur[s]])
        vmeta_v = vmeta[:].rearrange("p (h c) -> p h c", h=8)
        P.emit("dve", lambda e: e.memset(vmeta_v[0:NMETA, :, 64:128], 1.0), writes=[B_vmeta])

        class Stream:
            def __init__(self):
                self.units = []
                self.uslot = {}
                self.next_dma = 0
                self.head = 0
                self.free = list(range(NSLOT))

            def plan(self, lst):
                self.units.extend(lst)

            def _issue(self, k, slot):
                u = self.units[k]
                sl = slots[slot]
                sem = f"sl{slot}"
                if u[0] == "w":
                    _, idx, ncols = u
                    P.dma("pool", lambda e: e.dma_start(out=sl[:, 0:ncols], in_=wts_d[idx, :, 0:ncols]),
                          sem, writes=[B_slot[slot]])
                else:
                    _, h, jp0, n = u
                    rd = [B_kvk[jp] for jp in range(jp0, jp0 + n)] + [B_kvv[jp] for jp in range(jp0, jp0 + n)]
                    semk = f"sk{slot}"
                    P.dma("sp", lambda e: e.dma_start(out=sl[0:96, 0:n * T], in_=kvk_d[h, 0:96, jp0 * T:(jp0 + n) * T]),
                          semk, reads=rd, writes=[B_slot[slot]])
                    P.dma("sp", lambda e: e.dma_start(out=sl[:, 2048:2048 + n * T], in_=kvv_d[h, :, jp0 * T:(jp0 + n) * T]),
                          semk, reads=rd, writes=[B_slot[slot]])

            def pump(self):
                while self.free and self.next_dma < len(self.units):
                    slot = self.free.pop(0)
                    self.uslot[self.next_dma] = slot
                    self._issue(self.next_dma, slot)
                    self.next_dma += 1

            def acquire(self, *key):
                k = self.head
                assert tuple(self.units[k][:len(key)]) == tuple(key), (self.units[k], key)
                self.pump()
                assert k in self.uslot, "stream deadlock: no free slot"
                self.head += 1
                slot = self.uslot[k]
                return slots[slot], B_slot[slot], slot

            def release(self, slot):
                self.free.append(slot)
                self.pump()

        S = Stream()

        def kv_groups(j):
            g = []
            jp0 = 0
            while jp0 < j:
                n = min(4, j - jp0)
                g.append((jp0, n))
                jp0 += n
            return g

        def tile_units(j):
            u = [("w", U_INB, 4096), ("w", U_INC, 2048), ("w", U_INA, 4096),
                 ("w", U_UQ, 3 * 1152), ("w", U_UKV, 2048)]
            for h in range(8):
                for (jp0, n) in kv_groups(max(j, 0)):
                    u.append(("kv", h, jp0, n))
            u += [("w", U_PP, 4096), ("w", U_PMLA, 4096)]
            u += [("w", U_G0 + i, 4096) for i in range(4)]
            u += [("w", U_WO + i, 4096) for i in range(2)]
            if j >= 0:
                u += [("w", U_UP + i, 4096) for i in range(11)]
                u += [("w", U_DN + i, 4096) for i in range(6)]
            return u

        for j in range(-1, NT):
            S.plan(tile_units(j))

        bank_state = {"i": 0, "lo": 0, "hi": 8}

        def nbank():
            lo, hi = bank_state["lo"], bank_state["hi"]
            i = bank_state["i"]
            if i < lo or i >= hi:
                i = lo
            bank_state["i"] = i + 1 if i + 1 < hi else lo
            return i

        def mm(out, lhsT, rhs, start, stop, reads, writes, signal):
            P.emit("pe", lambda e: e.matmul(out, lhsT=lhsT, rhs=rhs, start=start, stop=stop),
                   reads=reads, writes=writes, signal=signal)

        evac_rr = {"i": 0}

        def copy_evac(out, in_, reads, writes, force=None):
            evac_rr["i"] ^= 1
            if force == "act" or (force is None and evac_rr["i"]):
                P.emit("act", lambda e: e.activation(out=out, in_=in_, func=AF.Copy), reads=reads, writes=writes)
            else:
                P.emit("dve", lambda e: e.tensor_copy(out=out, in_=in_), reads=reads, writes=writes)

        def ln_normalize(h, b, R):
            P.emit("dve", lambda e: e.bn_stats(out=stat[0:R, 0:6], in_=h[0:R, 0:512]), reads=[b], writes=[B_stat])
            P.emit("dve", lambda e: e.bn_stats(out=stat[0:R, 6:12], in_=h[0:R, 512:1024]), reads=[b], writes=[B_stat])
            P.emit("dve", lambda e: e.bn_aggr(out=stat[0:R, 12:14], in_=stat[0:R, 0:12]), reads=[B_stat], writes=[B_stat])
            P.emit("act", lambda e: e.activation(out=stat[0:R, 14:15], in_=stat[0:R, 13:14], func=AF.Ln,
                                                 bias=consts[0:R, C_EPS:C_EPS + 1], scale=1.0),
                   reads=[B_stat, B_consts], writes=[B_stat])
            P.emit("act", lambda e: e.activation(out=stat[0:R, 15:16], in_=stat[0:R, 14:15], func=AF.Exp, scale=-0.5),
                   reads=[B_stat], writes=[B_stat])
            P.emit("dve", lambda e: e.tensor_scalar(out=stat[0:R, 16:17], in0=stat[0:R, 12:13], scalar1=stat[0:R, 15:16],
                                                    scalar2=-1.0, op0=ALU.mult, op1=ALU.mult),
                   reads=[B_stat], writes=[B_stat])
            P.emit("act", lambda e: e.activation(out=h[0:R, :], in_=h[0:R, :], func=AF.Identity,
                                                 bias=stat[0:R, 16:17], scale=stat[0:R, 15:16]),
                   reads=[b, B_stat], writes=[b])

        def ln_affine(h, b, R, gi, aff, extra=()):
            P.emit(aff, lambda e: e.tensor_tensor(out=h[0:R, :], in0=h[0:R, :], in1=lnbc[gi][0:R, :], op=ALU.mult),
                   reads=[b, B_lnbc] + list(extra), writes=[b])
            P.emit(aff, lambda e: e.tensor_tensor(out=h[0:R, :], in0=h[0:R, :], in1=lnbc[gi + 1][0:R, :], op=ALU.add),
                   reads=[b, B_lnbc], writes=[b])

        def layer_norm(h, b, R, gi, aff="dve"):
            ln_normalize(h, b, R)
            ln_affine(h, b, R, gi, aff)

        def transpose_to_hT(src, bsrc, NS, R, Tt, lt, to_meta=False):
            for c in range(8):
                bk = nbank()
                for s in range(NS):
                    P.emit("pe", lambda e, s=s, c=c, bk=bk: e.transpose(psum[bk][:, s * 128:s * 128 + R],
                                                                       src[s][0:R, c * 128:(c + 1) * 128], ident[0:R, 0:R]),
                           reads=[bsrc[s], B_ident], writes=[B_ps[bk]], signal=(s == NS - 1))
                gcol = consts[:, C_LT + lt * 8 + c:C_LT + lt * 8 + c + 1]
                bcol = consts[:, C_LT + (lt + 1) * 8 + c:C_LT + (lt + 1) * 8 + c + 1]
                if to_meta:
                    dsto, bdst = hTm[:, c * NMETA:(c + 1) * NMETA], B_hTm
                else:
                    dsto, bdst = hT[:, c * T:c * T + Tt], B_hT[c]
                if c % 2 == 0:
                    P.emit("act", lambda e, dsto=dsto, bk=bk, gcol=gcol, bcol=bcol: e.activation(
                        out=dsto, in_=psum[bk][:, 0:Tt], func=AF.Identity, bias=bcol, scale=gcol),
                        reads=[B_ps[bk], B_consts], writes=[bdst])
                else:
                    P.emit("dve", lambda e, dsto=dsto, bk=bk, gcol=gcol, bcol=bcol: e.tensor_scalar(
                        out=dsto, in0=psum[bk][:, 0:Tt], scalar1=gcol, scalar2=bcol, op0=ALU.mult, op1=ALU.add),
                        reads=[B_ps[bk], B_consts], writes=[bdst])

        out_events = []

        def dump_tm():
            for s in range(4):
                ev = P.dma("sp", lambda e, s=s: e.dma_start(out=out_d[s * 128:(s + 1) * 128, :], in_=htm[s][:, :]), f"out{s}", reads=[B_htm[s]])
                out_events.append(ev)

        def dump_fm(ap, bufs, f0):
            for tc in range(8):
                ev = P.dma("pool", lambda e, tc=tc: e.dma_start(out=out_d[tc * 64:(tc + 1) * 64, f0:f0 + 128].rearrange("t f -> f t"), in_=ap[:, tc * 64:(tc + 1) * 64], allow_slow_non_contiguous=True), "outd", reads=bufs)
                out_events.append(ev)
        def tp(j):
            is_meta = j < 0
            return dict(is_meta=is_meta, Tt=NMETA if is_meta else T, NS=1 if is_meta else 4,
                        R=NMETA if is_meta else 128, pos0=0 if is_meta else NMETA + j * T)

        ra, rb = arena2[8], arena2[9]

        def phase_load_ln_in(j):
            c = tp(j)
            Tt, NS, R, pos0 = c["Tt"], c["NS"], c["R"], c["pos0"]
            if c["is_meta"]:
                P.dma("sp", lambda e: e.dma_start(out=xn[0][0:NMETA, :], in_=meta_d), "xin0", writes=[B_xn[0]])
            else:
                for s in range(NS):
                    P.dma("sp", lambda e, s=s: e.dma_start(out=xn[s][:, :], in_=x_d[j * T + s * 128:j * T + (s + 1) * 128, :]),
                          f"xin{s}", writes=[B_xn[s]])
            P.dma("sp", lambda e: e.dma_start(out=ropet[:, 0:Tt], in_=rope_d[0, :, pos0:pos0 + Tt]), "rope", writes=[B_ropet])
            P.dma("sp", lambda e: e.dma_start(out=ropet[:, T:T + Tt], in_=rope_d[1, :, pos0:pos0 + Tt]), "rope", writes=[B_ropet])
            for s in range(NS):
                ln_normalize(xn[s], B_xn[s], R)

        def phase_front(j):
            c = tp(j)
            is_meta, Tt, NS, R = c["is_meta"], c["Tt"], c["NS"], c["R"]
            bank_state["lo"], bank_state["hi"] = 0, 8

            def hTc(kc):
                return hT[:, kc * T:kc * T + Tt]

            cc = ropet[:, 0:Tt]
            ss = ropet[:, T:T + Tt]

            def rms_part1(wq, bq, ncols, chunks, ar0, sq0):
                for ci in range(chunks):
                    bk = nbank()
                    for kc in range(8):
                        mm(psum[bk][:, 0:Tt], wq[:, kc * ncols + ci * 128:kc * ncols + (ci + 1) * 128], hTc(kc),
                           kc == 0, kc == 7, [bq, B_hT[kc]], [B_ps[bk]], kc == 7)
                    a = arena2[ar0 + ci]
                    P.emit("act", lambda e, a=a, bk=bk: e.activation(out=a[:, 0:Tt], in_=psum[bk][:, 0:Tt], func=AF.Copy),
                           reads=[B_ps[bk]], writes=[B_ar2[ar0 + ci]])
                    P.emit("act", lambda e, ci=ci, bk=bk: e.activation(out=sqs[:, (sq0 + ci) * T:(sq0 + ci) * T + Tt], in_=psum[bk][:, 0:Tt], func=AF.Square),
                           reads=[B_ps[bk]], writes=B_sqs[sq0 + ci])

            def rms_part2(chunks, gcol, nfeat, ar0, cn0, rstd_i, sq0):
                bk2 = nbank()
                for ci in range(chunks):
                    mm(psum[bk2][:, 0:Tt], ones[:, :], sqs[:, (sq0 + ci) * T:(sq0 + ci) * T + Tt], ci == 0, ci == chunks - 1,
                       [B_ones] + B_sqs[sq0 + ci], [B_ps[bk2]], ci == chunks - 1)
                rs = arena2[rstd_i]
                P.emit("act", lambda e: e.activation(out=rs[:, 0:Tt], in_=psum[bk2][:, 0:Tt], func=AF.Ln,
                                                     bias=consts[:, C_EPS:C_EPS + 1], scale=1.0 / nfeat),
                       reads=[B_ps[bk2], B_consts], writes=[B_ar2[rstd_i]])
                P.emit("act", lambda e: e.activation(out=rs[:, 0:Tt], in_=rs[:, 0:Tt], func=AF.Exp, scale=-0.5),
                       reads=[B_ar2[rstd_i]], writes=[B_ar2[rstd_i]])
                for ci in range(chunks):
                    a = arena2[ar0 + ci]
                    P.emit("dve", lambda e, a=a, ci=ci: e.scalar_tensor_tensor(
                        out=cn[:, (cn0 + ci) * T:(cn0 + ci) * T + Tt], in0=a[:, 0:Tt], scalar=consts[:, gcol + ci:gcol + ci + 1],
                        in1=rs[:, 0:Tt], op0=ALU.mult, op1=ALU.mult),
                        reads=[B_ar2[ar0 + ci], B_ar2[rstd_i], B_consts], writes=[B_cn[cn0 + ci]])

            wB, bwB, slB = S.acquire("w", U_INB)
            rms_part1(wB, bwB, 512, 3, 0, 0)
            bkr = nbank()
            for kc in range(8):
                mm(psum[bkr][:, 0:Tt], wB[:, kc * 512 + 384:kc * 512 + 512], hTc(kc), kc == 0, kc == 7,
                   [bwB, B_hT[kc]], [B_ps[bkr]], kc == 7)
            S.release(slB)
            P.emit("dve", lambda e: e.tensor_tensor(out=ra[0:32, 0:Tt], in0=psum[bkr][0:32, 0:Tt], in1=cc, op=ALU.mult),
                   reads=[B_ps[bkr], B_ropet], writes=[B_ar2[8]])
            P.emit("dve", lambda e: e.tensor_tensor(out=rb[0:32, 0:Tt], in0=psum[bkr][32:64, 0:Tt], in1=ss, op=ALU.mult),
                   reads=[B_ps[bkr], B_ropet], writes=[B_ar2[9]])
            P.emit("dve", lambda e: e.tensor_tensor(out=ra[0:32, 0:Tt], in0=ra[0:32, 0:Tt], in1=rb[0:32, 0:Tt], op=ALU.add),
                   reads=[B_ar2[8], B_ar2[9]], writes=[B_ar2[8]])
            for h in range(8):
                if is_meta:
                    dst, bd = kmeta[64:96, h * 128:h * 128 + NMETA], B_kmeta
                else:
                    dst, bd = kcur[64:96, h * T:(h + 1) * T], B_kcur[h]
                copy_evac(dst, ra[0:32, 0:Tt], [B_ar2[8]], [bd], force="act")
            wC, bwC, slC = S.acquire("w", U_INC)
            rms_part1(wC, bwC, 256, 2, 3, 3)
            S.release(slC)

            w, bw, sl = S.acquire("w", U_INA)
            for g in range(4):
                bk = nbank()
                for kc in range(8):
                    mm(psum[bk][:, 0:Tt], w[:, kc * 512 + g * 128:kc * 512 + (g + 1) * 128], hTc(kc),
                       kc == 0, kc == 7, [bw, B_hT[kc]], [B_ps[bk]], kc == 7)
                P.emit("act", lambda e, g=g, bk=bk: e.activation(out=vp[:, g * 528 + 16:g * 528 + 16 + Tt],
                                                               in_=psum[bk][:, 0:Tt], func=AF.Copy),
                       reads=[B_ps[bk]], writes=[B_vp[g]])
            S.release(sl)

            rms_part2(3, C_QG, 384.0, 0, 0, 6, 0)
            rms_part2(2, C_KG, 256.0, 3, 3, 7, 3)
            wq, bwq, slq = S.acquire("w", U_UQ)
            NQ = 1152
            swb = []
            bank_state["lo"], bank_state["hi"] = 0, 3
            for gq in range(3):
                bk = nbank()
                swb.append(bk)
                for kc in range(3):
                    mm(psum[bk][:, 0:Tt], wq[:, kc * NQ + 768 + gq * 128:kc * NQ + 768 + (gq + 1) * 128],
                       cn[:, kc * T:kc * T + Tt], kc == 0, kc == 2, [bwq, B_cn[kc]], [B_ps[bk]], kc == 2)
            bank_state["lo"], bank_state["hi"] = 3, 8
            for h in range(8):
                bk = nbank()
                assert bk not in swb
                for kc in range(3):
                    mm(psum[bk][0:96, 0:Tt], wq[:, kc * NQ + h * 96:kc * NQ + (h + 1) * 96],
                       cn[:, kc * T:kc * T + Tt], kc == 0, kc == 2, [bwq, B_cn[kc]], [B_ps[bk]], kc == 2)
                P.emit("act", lambda e, h=h, bk=bk: e.activation(out=arena[0:64, h * T:h * T + Tt], in_=psum[bk][0:64, 0:Tt], func=AF.Copy),
                       reads=[B_ps[bk]], writes=[B_ar[h]])
                sbk = swb[h // 3]
                i3 = h % 3
                P.emit("dve", lambda e, bk=bk: e.tensor_tensor(out=ra[0:32, 0:Tt], in0=psum[bk][64:96, 0:Tt], in1=cc, op=ALU.mult),
                       reads=[B_ps[bk], B_ropet], writes=[B_ar2[8]])
                P.emit("dve", lambda e, sbk=sbk, i3=i3: e.tensor_tensor(out=rb[0:32, 0:Tt], in0=psum[sbk][i3 * 32:(i3 + 1) * 32, 0:Tt], in1=ss, op=ALU.mult),
                       reads=[B_ps[sbk], B_ropet], writes=[B_ar2[9]])
                P.emit("dve", lambda e, h=h: e.tensor_tensor(out=arena[64:96, h * T:h * T + Tt], in0=ra[0:32, 0:Tt], in1=rb[0:32, 0:Tt], op=ALU.add),
                       reads=[B_ar2[8], B_ar2[9]], writes=[B_ar[h]])
            S.release(slq)
            bank_state["lo"], bank_state["hi"] = 0, 8
            wkv, bwkv, slkv = S.acquire("w", U_UKV)
            for hp in range(4):
                bk = nbank()
                for kc in range(2):
                    mm(psum[bk][:, 0:Tt], wkv[:, kc * 1024 + hp * 128:kc * 1024 + (hp + 1) * 128],
                       cn[:, (3 + kc) * T:(3 + kc) * T + Tt], kc == 0, kc == 1, [bwkv, B_cn[3 + kc]], [B_ps[bk]], kc == 1)
                for hh in range(2):
                    h = 2 * hp + hh
                    if is_meta:
                        dst, bd = kmeta[0:64, h * 128:h * 128 + NMETA], B_kmeta
                    else:
                        dst, bd = kcur[0:64, h * T:(h + 1) * T], B_kcur[h]
                    copy_evac(dst, psum[bk][hh * 64:(hh + 1) * 64, 0:Tt], [B_ps[bk]], [bd], force="act")
            for s in range(NS):
                bk = nbank()
                for kc in range(2):
                    mm(psum[bk][0:R, :], cn[:, (3 + kc) * T + s * 128:(3 + kc) * T + s * 128 + R],
                       wkv[:, kc * 1024 + 512:kc * 1024 + 1024], kc == 0, kc == 1, [bwkv, B_cn[3 + kc]], [B_ps[bk]], kc == 1)
                src = psum[bk][0:R, :].rearrange("p (h c) -> p h c", h=8)
                if is_meta:
                    dst, bd = vmeta_v[0:R, :, 0:64], B_vmeta
                else:
                    dst, bd = vcur_v[:, :, s, 0:64], B_vcur[s]
                copy_evac(dst, src, [B_ps[bk]], [bd], force="act")
            S.release(slkv)
            if not is_meta and j < NT - 1:
                P.dma("sp", lambda e: e.dma_start(out=kvk_d[:, 0:96, j * T:(j + 1) * T].rearrange("h p t -> p h t"),
                                                  in_=kcur[0:96, :].rearrange("p (h t) -> p h t", h=8)),
                      "kvstk", reads=B_kcur, writes=[B_kvk[j]])
                P.dma("sp", lambda e: e.dma_start(out=kvv_d[:, :, j * T:(j + 1) * T].rearrange("h p t -> p h t"),
                                                  in_=vcur[:, :].rearrange("p (h t) -> p h t", h=8)),
                      "kvstv", reads=B_vcur, writes=[B_kvv[j]])

        def phase_pool(j):
            c = tp(j)
            is_meta, Tt = c["is_meta"], c["Tt"]
            wins = (2, 4, 8, 16)
            for g in range(4):
                v = vp[:, g * 528:(g + 1) * 528]
                bv = B_vp[g]
                n = 16 + Tt
                src, bsrc = v, bv
                lo = 0
                step = 1
                ti = 0
                while step < wins[g]:
                    dst, bdst = ptmp[ti], B_ptmp[ti]
                    nlo = lo + step
                    P.emit("dve", lambda e, dst=dst, src=src, nlo=nlo, step=step, n=n: e.tensor_tensor(
                        out=dst[:, nlo:n], in0=src[:, nlo:n], in1=src[:, nlo - step:n - step], op=ALU.add),
                        reads=[bsrc], writes=[bdst])
                    src, bsrc, lo = dst, bdst, nlo
                    step *= 2
                    ti ^= 1
                if is_meta:
                    P.emit("dve", lambda e, src=src, g=g: e.tensor_tensor(
                        out=src[:, 16:16 + Tt], in0=src[:, 16:16 + Tt], in1=consts[:, C_IC + g * 16:C_IC + (g + 1) * 16], op=ALU.mult),
                        reads=[bsrc, B_consts], writes=[bsrc])
                    P.emit("dve", lambda e, src=src, v=v, g=g: e.tensor_tensor(
                        out=pmin[:, g * T:g * T + Tt], in0=src[:, 16:16 + Tt], in1=v[:, 16:16 + Tt], op=ALU.subtract),
                        reads=[bsrc, bv], writes=[B_pmin[g]])
                else:
                    P.emit("dve", lambda e, src=src, v=v, g=g: e.scalar_tensor_tensor(
                        out=pmin[:, g * T:g * T + Tt], in0=src[:, 16:16 + Tt], scalar=1.0 / wins[g], in1=v[:, 16:16 + Tt],
                        op0=ALU.mult, op1=ALU.subtract),
                        reads=[bsrc, bv], writes=[B_pmin[g]])
                P.emit("act", lambda e, v=v: e.activation(out=v[:, 0:16], in_=v[:, Tt:Tt + 16], func=AF.Copy), reads=[bv], writes=[bv])

        def phase_attn(j, hook=None):
            c = tp(j)
            is_meta, Tt = c["is_meta"], c["Tt"]
            bank_state["lo"], bank_state["hi"] = 2, 8
            PT = [(arena[:, 20 * T:21 * T], B_ar[20]), (arena[:, 21 * T:22 * T], B_ar[21]), (pt3[:, :], B_pt3)]
            pti = {"i": 0}
            LA = 2
            pend = []

            def attn_item(h, K, bK, V, bV, n0, first, last, obk, rel):
                sbk = nbank()
                q = arena[0:96, h * T + n0:h * T + Tt]
                mm(psum[sbk][:, n0:Tt], K, q, True, True, [bK, B_ar[h]], [B_ps[sbk]], True)
                pt, bpt = PT[pti["i"]]
                pti["i"] = (pti["i"] + 1) % 3

                def tail():
                    P.emit("act", lambda e: e.activation(out=pt[:, n0:Tt], in_=psum[sbk][:, n0:Tt], func=AF.Exp, scale=ATTN_SCALE),
                           reads=[B_ps[sbk]], writes=[bpt])
                    if rel:
                        P.emit("act", lambda e: e.memzero(pt[64:128, n0:n0 + 64]), writes=[bpt])
                    mm(psum[obk][:, n0:Tt], V, pt[:, n0:Tt], first, last, [bV, bpt], [B_ps[obk]], True)
                    if last:
                        P.emit("dve", lambda e: e.reciprocal(out=rec[0:64, 0:Tt], in_=psum[obk][64:128, 0:Tt]),
                               reads=[B_ps[obk]], writes=[B_rec])
                        hp, hh = h // 2, h % 2
                        P.emit("dve", lambda e: e.tensor_tensor(out=arena[hh * 64:(hh + 1) * 64, (16 + hp) * T:(16 + hp) * T + Tt],
                                                                in0=psum[obk][0:64, 0:Tt], in1=rec[0:64, 0:Tt], op=ALU.mult),
                               reads=[B_ps[obk], B_rec], writes=[B_ar[16 + hp]])
                pend.append(tail)
                if len(pend) > LA:
                    pend.pop(0)()

            for h in range(8):
                if h == 2 and hook is not None:
                    hook()
                obk = h % 2
                items = [(kmeta[0:96, h * 128:(h + 1) * 128], B_kmeta, vmeta[:, h * 128:(h + 1) * 128], B_vmeta, 0, False, None)]
                for (jp0, n) in kv_groups(max(j, 0)):
                    ks, bks, slk = S.acquire("kv", h, jp0, n)
                    for bi in range(4 * n):
                        items.append((ks[0:96, bi * 128:(bi + 1) * 128], bks, ks[:, 2048 + bi * 128:2048 + (bi + 1) * 128], bks, 0,
                                      False, slk if bi == 4 * n - 1 else None))
                if not is_meta:
                    for kb in range(4):
                        items.append((kcur[0:96, h * T + kb * 128:h * T + (kb + 1) * 128], B_kcur[h],
                                      vcur[:, (h * 4 + kb) * 128:(h * 4 + kb + 1) * 128], B_vcur[kb], kb * 128, True, None))
                for ii, (K, bK, V, bV, n0, rel, relslot) in enumerate(items):
                    attn_item(h, K, bK, V, bV, n0, ii == 0, ii == len(items) - 1, obk, rel)
                    if relslot is not None:
                        while pend:
                            pend.pop(0)()
                        S.release(relslot)
            while pend:
                pend.pop(0)()
            bank_state["lo"], bank_state["hi"] = 0, 8

        def phase_merge_wout_ln1(j):
            c = tp(j)
            Tt, NS, R = c["Tt"], c["NS"], c["R"]

            def hTc(kc):
                return hT[:, kc * T:kc * T + Tt]

            for g in range(4):
                bk = nbank()
                mm(psum[bk][:, 0:Tt], poolw[:, g * 128:(g + 1) * 128], pmin[:, g * T:g * T + Tt], True, True,
                   [B_poolw, B_pmin[g]], [B_ps[bk]], True)
                P.emit("act", lambda e, g=g, bk=bk: e.activation(out=pm[:, g * T:g * T + Tt], in_=psum[bk][:, 0:Tt], func=AF.Identity,
                                                               scale=consts[:, C_PS + g:C_PS + g + 1]),
                       reads=[B_ps[bk], B_consts], writes=[B_pm[g]])
            wpp, bwpp, slpp = S.acquire("w", U_PP)
            wpm, bwpm, slpm = S.acquire("w", U_PMLA)
            for u in range(4):
                wg, bwg, slg = S.acquire("w", U_G0 + u)
                for mi in range(2):
                    m = 2 * u + mi
                    bky = nbank()
                    for g in range(4):
                        mm(psum[bky][:, 0:Tt], wpp[:, g * 1024 + m * 128:g * 1024 + (m + 1) * 128], pm[:, g * T:g * T + Tt],
                           g == 0, g == 3, [bwpp, B_pm[g]], [B_ps[bky]], g == 3)
                    bkg0 = nbank()
                    for kc in range(8):
                        mm(psum[bkg0][:, 0:Tt], wg[:, kc * 512 + mi * 256:kc * 512 + mi * 256 + 128], hTc(kc),
                           kc == 0, kc == 7, [bwg, B_hT[kc]], [B_ps[bkg0]], kc == 7)
                    bkm = nbank()
                    for hp in range(4):
                        mm(psum[bkm][:, 0:Tt], wpm[:, hp * 1024 + m * 128:hp * 1024 + (m + 1) * 128],
                           arena[:, (16 + hp) * T:(16 + hp) * T + Tt], hp == 0, hp == 3, [bwpm, B_ar[16 + hp]], [B_ps[bkm]], hp == 3)
                    bkg1 = nbank()
                    for kc in range(8):
                        mm(psum[bkg1][:, 0:Tt], wg[:, kc * 512 + mi * 256 + 128:kc * 512 + mi * 256 + 256], hTc(kc),
                           kc == 0, kc == 7, [bwg, B_hT[kc]], [B_ps[bkg1]], kc == 7)
                    P.emit("act", lambda e, m=m, bkg0=bkg0: e.activation(out=sg[0][:, 0:Tt], in_=psum[bkg0][:, 0:Tt], func=AF.Sigmoid,
                                                                       bias=consts[:, C_BG + m:C_BG + m + 1], scale=1.0),
                           reads=[B_ps[bkg0], B_consts], writes=[B_sg[0]])
                    P.emit("act", lambda e, m=m, bkg1=bkg1: e.activation(out=sg[1][:, 0:Tt], in_=psum[bkg1][:, 0:Tt], func=AF.Sigmoid,
                                                                       bias=consts[:, C_BG + 8 + m:C_BG + 8 + m + 1], scale=1.0),
                           reads=[B_ps[bkg1], B_consts], writes=[B_sg[1]])
                    P.emit("dve", lambda e, bky=bky: e.tensor_tensor(out=sg[0][:, 0:Tt], in0=sg[0][:, 0:Tt], in1=psum[bky][:, 0:Tt], op=ALU.mult),
                           reads=[B_sg[0], B_ps[bky]], writes=[B_sg[0]])
                    P.emit("dve", lambda e, bkm=bkm: e.tensor_tensor(out=sg[1][:, 0:Tt], in0=sg[1][:, 0:Tt], in1=psum[bkm][:, 0:Tt], op=ALU.mult),
                           reads=[B_sg[1], B_ps[bkm]], writes=[B_sg[1]])
                    P.emit("dve", lambda e, m=m: e.tensor_tensor(out=arena[:, (8 + m) * T:(8 + m) * T + Tt], in0=sg[0][:, 0:Tt], in1=sg[1][:, 0:Tt], op=ALU.add),
                           reads=[B_sg[0], B_sg[1]], writes=[B_ar[8 + m]])
                S.release(slg)
            S.release(slpp)
            S.release(slpm)
            wo0, bwo0, slo0 = S.acquire("w", U_WO)
            wo1, bwo1, slo1 = S.acquire("w", U_WO + 1)
            for s in range(NS):
                for half, (wo, bwo) in enumerate(((wo0, bwo0), (wo1, bwo1))):
                    bk = nbank()
                    for kc in range(8):
                        mm(psum[bk][0:R, :], arena[:, (8 + kc) * T + s * 128:(8 + kc) * T + s * 128 + R], wo[:, kc * 512:(kc + 1) * 512],
                           kc == 0, kc == 7, [bwo, B_ar[8 + kc]], [B_ps[bk]], kc == 7)
                    P.emit("dve", lambda e, s=s, half=half, bk=bk: e.scalar_tensor_tensor(
                        out=htm[s][0:R, half * 512:(half + 1) * 512], in0=xn[s][0:R, half * 512:(half + 1) * 512], scalar=ALPHA,
                        in1=psum[bk][0:R, :], op0=ALU.mult, op1=ALU.add),
                        reads=[B_xn[s], B_ps[bk]], writes=[B_htm[s]])
                ln_normalize(htm[s], B_htm[s], R)
            S.release(slo0)
            S.release(slo1)

        def phase_ffn_up(j):
            c = tp(j)
            is_meta, Tt = c["is_meta"], c["Tt"]
            rpar = j % 2 if j >= 0 else 0
            wpar = (j + 1) % 2

            def hTc(kc):
                return hT[:, kc * T:kc * T + Tt]

            dbl = 0
            for u in range(11):
                wu, bwu, slu = S.acquire("w", U_UP + u)
                for mi in range(2):
                    jc = 2 * u + mi
                    ab = [arena2[0 + dbl], arena2[2 + dbl]]
                    bab = [B_ar2[0 + dbl], B_ar2[2 + dbl]]
                    cb_ = [arena2[4 + dbl], arena2[6 + dbl]]
                    bcb = [B_ar2[4 + dbl], B_ar2[6 + dbl]]
                    sgb, bsgb = arena2[8 + dbl], B_ar2[8 + dbl]
                    acts, dves = [[], []], [[], []]
                    for br in range(2):
                        ch = br * NFC + jc
                        col0 = br * 256 + mi * 128
                        hold = ahalo[:, (rpar * 44 + ch) * 2:(rpar * 44 + ch) * 2 + 2]
                        hnew = ahalo[:, (wpar * 44 + ch) * 2:(wpar * 44 + ch) * 2 + 2]
                        if j == 0:
                            bkm = nbank()
                            for kc in range(8):
                                mm(psum[bkm][:, 0:NMETA], wu[:, kc * 512 + col0:kc * 512 + col0 + 128], hTm[:, kc * NMETA:(kc + 1) * NMETA],
                                   kc == 0, kc == 7, [bwu, B_hTm], [B_ps[bkm]], kc == 7)
                            P.emit("act", lambda e, hold=hold, bkm=bkm: e.activation(out=hold, in_=psum[bkm][:, NMETA - 2:NMETA], func=AF.Copy),
                                   reads=[B_ps[bkm]], writes=[B_ahalo[rpar * 44 + ch]])
                        bk = nbank()
                        for kc in range(8):
                            mm(psum[bk][:, 0:Tt], wu[:, kc * 512 + col0:kc * 512 + col0 + 128], hTc(kc),
                               kc == 0, kc == 7, [bwu, B_hT[kc]], [B_ps[bk]], kc == 7)
                        a, ba = ab[br], bab[br]
                        cc_, bc = cb_[br], bcb[br]
                        A, Dv = acts[br], dves[br]
                        A.append(lambda hnew=hnew, bk=bk, ch=ch: P.emit("act", lambda e: e.activation(out=hnew, in_=psum[bk][:, Tt - 2:Tt], func=AF.Copy),
                                                                        reads=[B_ps[bk]], writes=[B_ahalo[wpar * 44 + ch]]))
                        if is_meta:
                            continue
                        w0 = consts[:, C_CW + ch:C_CW + ch + 1]
                        w1 = consts[:, C_CW + 44 + ch:C_CW + 44 + ch + 1]
                        w2 = consts[:, C_CW + 88 + ch:C_CW + 88 + ch + 1]
                        bch = consts[:, C_CB + ch:C_CB + ch + 1]
                        rh = B_ahalo[rpar * 44 + ch]
                        if br == 0:
                            A.append(lambda a=a, hold=hold, ba=ba, rh=rh: P.emit("act", lambda e: e.activation(out=a[:, 0:2], in_=hold, func=AF.Copy),
                                                                             reads=[rh], writes=[ba]))
                            A.append(lambda a=a, bk=bk, ba=ba: P.emit("act", lambda e: e.activation(out=a[:, 2:2 + Tt], in_=psum[bk][:, 0:Tt], func=AF.Copy),
                                                                      reads=[B_ps[bk]], writes=[ba]))
                        A.append(lambda cc_=cc_, bk=bk, w2=w2, bch=bch, bc=bc: P.emit("act", lambda e: e.activation(
                            out=cc_[:, 0:Tt], in_=psum[bk][:, 0:Tt], func=AF.Identity, bias=bch, scale=w2),
                            reads=[B_ps[bk], B_consts], writes=[bc]))
                        if br == 0:
                            Dv.append(lambda a=a, cc_=cc_, w1=w1, ba=ba, bc=bc: P.emit("dve", lambda e: e.scalar_tensor_tensor(
                                out=cc_[:, 0:Tt], in0=a[:, 1:1 + Tt], scalar=w1, in1=cc_[:, 0:Tt],
                                op0=ALU.mult, op1=ALU.add), reads=[ba, bc, B_consts], writes=[bc]))
                            Dv.append(lambda a=a, cc_=cc_, w0=w0, ba=ba, bc=bc: P.emit("dve", lambda e: e.scalar_tensor_tensor(
                                out=cc_[:, 0:Tt], in0=a[:, 0:Tt], scalar=w0, in1=cc_[:, 0:Tt],
                                op0=ALU.mult, op1=ALU.add), reads=[ba, bc, B_consts], writes=[bc]))
                        else:
                            Dv.append(lambda cc_=cc_, bk=bk, w1=w1, bc=bc: P.emit("dve", lambda e: e.scalar_tensor_tensor(
                                out=cc_[:, 1:Tt], in0=psum[bk][:, 0:Tt - 1], scalar=w1, in1=cc_[:, 1:Tt],
                                op0=ALU.mult, op1=ALU.add), reads=[B_ps[bk], bc, B_consts], writes=[bc]))
                            Dv.append(lambda cc_=cc_, bk=bk, w0=w0, bc=bc: P.emit("dve", lambda e: e.scalar_tensor_tensor(
                                out=cc_[:, 2:Tt], in0=psum[bk][:, 0:Tt - 2], scalar=w0, in1=cc_[:, 2:Tt],
                                op0=ALU.mult, op1=ALU.add), reads=[B_ps[bk], bc, B_consts], writes=[bc]))
                            Dv.append(lambda cc_=cc_, hold=hold, w0=w0, bc=bc, rh=rh: P.emit("dve", lambda e: e.scalar_tensor_tensor(
                                out=cc_[:, 0:2], in0=hold, scalar=w0, in1=cc_[:, 0:2],
                                op0=ALU.mult, op1=ALU.add), reads=[rh, bc, B_consts], writes=[bc]))
                            Dv.append(lambda cc_=cc_, hold=hold, w1=w1, bc=bc, rh=rh: P.emit("dve", lambda e: e.scalar_tensor_tensor(
                                out=cc_[:, 0:1], in0=hold[:, 1:2], scalar=w1, in1=cc_[:, 0:1],
                                op0=ALU.mult, op1=ALU.add), reads=[rh, bc, B_consts], writes=[bc]))
                    for f in acts[0] + acts[1]:
                        f()
                    g_, u_ = dves
                    order = []
                    for k in range(max(len(g_), len(u_))):
                        if k < len(g_):
                            order.append(g_[k])
                        if k < len(u_):
                            order.append(u_[k])
                    for f in order:
                        f()
                    if not is_meta:
                        P.emit("act", lambda e, c0=cb_[0], sgb=sgb: e.activation(out=sgb[:, 0:Tt], in_=c0[:, 0:Tt], func=AF.Silu),
                               reads=[bcb[0]], writes=[bsgb])
                        P.emit("dve", lambda e, c1=cb_[1], jc=jc, sgb=sgb: e.tensor_tensor(out=arena[:, jc * T:(jc + 1) * T], in0=sgb[:, 0:Tt], in1=c1[:, 0:Tt], op=ALU.mult),
                               reads=[bsgb, bcb[1]], writes=[B_ar[jc]])
                    dbl ^= 1
                S.release(slu)

        def phase_wdown(j):
            for u in range(6):
                wd, bwd, sld = S.acquire("w", U_DN + u)
                for ki in range(4):
                    kc = 4 * u + ki
                    if kc >= NFC:
                        break
                    for s in range(4):
                        for half in range(2):
                            bk = s * 2 + half
                            mm(psum[bk][:, :], arena[:, kc * T + s * 128:kc * T + (s + 1) * 128], wd[:, ki * 1024 + half * 512:ki * 1024 + (half + 1) * 512],
                               kc == 0, kc == NFC - 1, [bwd, B_ar[kc]], [B_ps[bk]],
                               kc == NFC - 1 or (ki == 3 and bk == 7))
                S.release(sld)
            for s in range(4):
                for half in range(2):
                    bk = s * 2 + half
                    P.emit("dve", lambda e, s=s, half=half, bk=bk: e.scalar_tensor_tensor(
                        out=htm[s][:, half * 512:(half + 1) * 512], in0=htm[s][:, half * 512:(half + 1) * 512], scalar=ALPHA,
                        in1=psum[bk][:, :], op0=ALU.mult, op1=ALU.add),
                        reads=[B_htm[s], B_ps[bk]], writes=[B_htm[s]])

        def phase_ln2_out(j):
            for s in range(4):
                layer_norm(htm[s], B_htm[s], 128, 4, "pool")
                ev = P.dma("sp", lambda e, s=s: e.dma_start(out=out_d[j * T + s * 128:j * T + (s + 1) * 128, :], in_=htm[s][:, :]),
                           f"out{s}", reads=[B_htm[s]])
                out_events.append(ev)

        def aff_all(buf, bbuf, NS, R, gi, extra=()):
            for s in range(NS):
                ln_affine(buf[s], bbuf[s], R, gi, "pool", extra)

        phase_load_ln_in(-1)
        transpose_to_hT(xn, B_xn, 1, NMETA, NMETA, 0)
        aff_all(xn, B_xn, 1, NMETA, 0)
        phase_front(-1)
        phase_attn(-1, lambda: phase_pool(-1))
        phase_merge_wout_ln1(-1)
        transpose_to_hT(htm, B_htm, 1, NMETA, NMETA, 2, to_meta=True)
        phase_load_ln_in(0)
        transpose_to_hT(xn, B_xn, 4, 128, T, 0)
        aff_all(xn, B_xn, 4, 128, 0)
        for j in range(NT):
            phase_front(j)

            def hook(j=j):
                phase_pool(j)
                if j > 0:
                    phase_ln2_out(j - 1)
            phase_attn(j, hook)
            phase_merge_wout_ln1(j)
            transpose_to_hT(htm, B_htm, 4, 128, T, 2)
            if j + 1 < NT:
                phase_load_ln_in(j + 1)
            phase_ffn_up(j)
            aff_all(htm, B_htm, 4, 128, 2, extra=[B_ar[NFC - 1]])
            if j + 1 < NT:
                transpose_to_hT(xn, B_xn, 4, 128, T, 0)
                aff_all(xn, B_xn, 4, 128, 0)
            phase_wdown(j)
        phase_ln2_out(NT - 1)
        last = {}
        for src, val in out_events:
            last[src] = max(last.get(src, 0), val)
        P.wait_all("sp", list(last.items()))
        P.build(nc, st)
    print(f"[kernel] NT={NT} ops={P.nops} waits={P.nwaits} per-engine={ {e: len(P.ops[e]) for e in P.ENG} }", flush=True)
    return nc


def _unit(arr2d, kc, ncols_pad=None):
    n = arr2d.shape[1]
    a = arr2d.reshape(kc, 128, n).transpose(1, 0, 2).reshape(128, kc * n)
    out = np.zeros((128, USZ), np.float32)
    out[:, :kc * n] = a
    return out


def prep_shared(inp, LTOT):
    f = np.float32
    w_in = np.asarray(inp["w_in"][0], f)
    units = np.zeros((NUNITS, 128, USZ), f)
    units[U_INA] = _unit(w_in[:, 0:512], 8)
    B = np.zeros((1024, 512), f)
    B[:, 0:384] = w_in[:, 512:896]
    B[:, 384:416] = w_in[:, 1152:1184]
    B[:, 416:432] = w_in[:, 1168:1184]
    B[:, 432:448] = w_in[:, 1152:1168]
    units[U_INB] = _unit(B, 8)
    units[U_INC] = _unit(w_in[:, 896:1152], 8)
    units[U_PP] = _unit(np.asarray(inp["p_pool"][0], f), 4)
    wuq = np.asarray(inp["w_uq"][0], f)
    Q = np.zeros((384, 1152), f)
    Q[:, 0:768] = wuq.reshape(384, 768)
    for h in range(8):
        g, i = h // 3, h % 3
        c0 = 768 + g * 128 + i * 32
        Q[:, c0:c0 + 16] = wuq[:, h, 80:96]
        Q[:, c0 + 16:c0 + 32] = wuq[:, h, 64:80]
    units[U_UQ] = _unit(Q, 3)
    KV = np.concatenate([np.asarray(inp["w_uk"][0], f).reshape(256, 512), np.asarray(inp["w_uv"][0], f).reshape(256, 512)], axis=1)
    units[U_UKV] = _unit(KV, 2)
    units[U_PMLA] = _unit(np.asarray(inp["p_mla"][0], f), 4)
    for u in range(4):
        G = np.zeros((1024, 512), f)
        for mi in range(2):
            m = 2 * u + mi
            G[:, mi * 256:mi * 256 + 128] = w_in[:, 1184 + m * 128:1184 + (m + 1) * 128]
            G[:, mi * 256 + 128:mi * 256 + 256] = w_in[:, 1184 + 1024 + m * 128:1184 + 1024 + (m + 1) * 128]
        units[U_G0 + u] = _unit(G, 8)
    w_out = np.asarray(inp["w_out"][0], f)
    for half in range(2):
        units[U_WO + half] = _unit(w_out[:, half * 512:(half + 1) * 512], 8)
    w_up = np.asarray(inp["w_ffn_up"][0], f)
    for u in range(11):
        Ub = np.zeros((1024, 512), f)
        for mi in range(2):
            jc = 2 * u + mi
            Ub[:, mi * 128:(mi + 1) * 128] = w_up[:, jc * 128:(jc + 1) * 128]
            Ub[:, 256 + mi * 128:256 + (mi + 1) * 128] = w_up[:, DFF + jc * 128:DFF + (jc + 1) * 128]
        units[U_UP + u] = _unit(Ub, 8)
    w_dn = np.asarray(inp["w_ffn_down"][0], f)
    for u in range(6):
        k0, k1 = 4 * u, min(4 * u + 4, NFC)
        Dn = np.zeros((512, 1024), f)
        Dn[:(k1 - k0) * 128] = w_dn[k0 * 128:k1 * 128]
        units[U_DN + u] = _unit(Dn, 4)
    poolw = np.ascontiguousarray(np.asarray(inp["pool_w"][0], f).transpose(1, 0, 2).reshape(128, 512))
    lnbc = np.stack([np.broadcast_to(np.asarray(a, f).reshape(1, D), (128, D)) for a in
                     (inp["ln_in_g"], inp["ln_in_b"], inp["ln1_g"][0], inp["ln1_b"][0], inp["ln2_g"][0], inp["ln2_b"][0])]).copy()
    consts = np.zeros((128, NCONST), f)
    consts[:, C_BG:C_BG + 16] = np.asarray(inp["b_gate"][0], f).reshape(16, 128).T
    cw = np.asarray(inp["ffn_conv_w"][0], f)
    for k in range(3):
        consts[:, C_CW + k * 44:C_CW + (k + 1) * 44] = cw[k].reshape(44, 128).T
    consts[:, C_CB:C_CB + 44] = np.asarray(inp["ffn_conv_b"][0], f).reshape(44, 128).T
    consts[:, C_PS:C_PS + 4] = np.asarray(inp["pool_scale"][0], f).reshape(4, 128).T
    consts[:, C_QG:C_QG + 3] = np.asarray(inp["q_norm_g"][0], f).reshape(3, 128).T
    consts[:, C_KG:C_KG + 2] = np.asarray(inp["kv_norm_g"][0], f).reshape(2, 128).T
    for g, wv in enumerate((2, 4, 8, 16)):
        consts[:, C_IC + g * 16:C_IC + (g + 1) * 16] = (1.0 / np.minimum(np.arange(16) + 1, wv)).astype(f)[None, :]
    consts[:, C_EPS] = EPS
    for wi, a in enumerate((inp["ln_in_g"], inp["ln_in_b"], inp["ln1_g"][0], inp["ln1_b"][0])):
        consts[:, C_LT + wi * 8:C_LT + (wi + 1) * 8] = np.asarray(a, f).reshape(8, 128).T
    pos = np.arange(LTOT, dtype=f)
    inv = (f(10000.0) ** (-np.arange(0, 32, 2, dtype=f) / f(32))).astype(f)
    ang = (pos[None, :] * inv[:, None]).astype(f)
    cosv, sinv = np.cos(ang).astype(f), np.sin(ang).astype(f)
    rope = np.stack([np.concatenate([cosv, cosv], 0), np.concatenate([-sinv, sinv], 0)]).astype(f)
    return dict(meta=np.ascontiguousarray(np.asarray(inp["meta"], f)), lnbc=lnbc, consts=consts,
                ident=np.eye(128, dtype=f), rope=np.ascontiguousarray(rope), poolw=poolw, wts=units)


_CACHE = {}


def kernel(**inputs):
    x = np.asarray(inputs["x"], np.float32)
    Bn, Sq, _ = x.shape
    NT = Sq // T
    shared = prep_shared(inputs, NMETA + NT * T)
    if NT not in _CACHE:
        _CACHE[NT] = build_program(NT)
    nc = _CACHE[NT]
    in_maps = [dict(shared, x=np.ascontiguousarray(x[b])) for b in range(Bn)]
    res = run_bass_kernel_spmd(nc, in_maps, core_ids=list(range(Bn)))
    return np.stack([np.asarray(r["out"], np.float32) for r in res.results], axis=0)
```

```python
import numpy as np
from contextlib import ExitStack
import concourse.bass as bass
import concourse.mybir as mybir
from concourse.bass_utils import run_bass_kernel_spmd

F32 = mybir.dt.float32
BF16 = mybir.dt.bfloat16
ALU = mybir.AluOpType
AF = mybir.ActivationFunctionType

D = 1024
NMETA = 16
T = 512
EPS = 1e-6
ALPHA = 2.0 ** 0.25
ATTN_SCALE = 96.0 ** -0.5
DFF = 2816
NFC = 22
USZ = 4096
NSLOT = 5

U_INA, U_INB, U_INC = 0, 1, 2
U_PP, U_UQ, U_UKV, U_PMLA = 3, 4, 5, 6
U_G0 = 7
U_WO = 11
U_UP = 13
U_DN = 24
NUNITS = 30

C_BG = 0
C_CW = 16
C_CB = C_CW + 132
C_PS = C_CB + 44
C_QG = C_PS + 4
C_KG = C_QG + 3
C_IC = C_KG + 2
C_EPS = C_IC + 64
C_LT = C_EPS + 1
NCONST = C_LT + 32


class Buf:
    __slots__ = ("name", "w", "r")

    def __init__(self, name):
        self.name = name
        self.w = {}
        self.r = {}


class Prog:
    ENG = ("pe", "act", "dve", "pool", "sp")

    def __init__(self):
        self.ops = {e: [] for e in self.ENG}
        self.cnt = {}
        self.seen = {e: {} for e in self.ENG}
        self.nwaits = 0
        self.nops = 0

    def _deps(self, eng, reads, writes):
        deps = {}

        def add(src, val, raw):
            if src == eng and eng == "pe":
                return
            if deps.get(src, 0) < val:
                deps[src] = val

        for b in reads:
            for src, val in b.w.items():
                add(src, val, True)
        for b in writes:
            for src, val in b.w.items():
                add(src, val, False)
            for src, val in b.r.items():
                add(src, val, False)
        waits = []
        seen = self.seen[eng]
        for src, val in deps.items():
            if seen.get(src, 0) < val:
                seen[src] = val
                waits.append((src, val))
        self.nwaits += len(waits)
        return waits

    def emit(self, eng, fn, reads=(), writes=(), signal=True):
        waits = self._deps(eng, reads, writes)
        for src, val in waits:
            if src in self.ENG and val > self.cnt.get(src, 0):
                raise RuntimeError(f"wait on unsignaled event {src} {val} > {self.cnt.get(src, 0)}")
        if signal:
            self.cnt[eng] = self.cnt.get(eng, 0) + 1
            v = self.cnt[eng]
        else:
            v = self.cnt.get(eng, 0) + 1
        for b in reads:
            if b.r.get(eng, 0) < v:
                b.r[eng] = v
        for b in writes:
            if b.w.get(eng, 0) < v:
                b.w[eng] = v
        self.ops[eng].append((waits, fn, eng if signal else None, 1))
        self.nops += 1
        return (eng, v)

    def dma(self, qeng, fn, sem, reads=(), writes=()):
        waits = self._deps(qeng, reads, writes)
        self.cnt[sem] = self.cnt.get(sem, 0) + 16
        v = self.cnt[sem]
        for b in reads:
            b.r[sem] = v
        for b in writes:
            b.w[sem] = v
        self.ops[qeng].append((waits, fn, sem, 16))
        self.nops += 1
        return (sem, v)

    def wait_all(self, eng, evs):
        waits = []
        for src, val in evs:
            if self.seen[eng].get(src, 0) < val:
                self.seen[eng][src] = val
                waits.append((src, val))
        self.ops[eng].append((waits, None, None, 0))

    def build(self, nc, stack):
        names = sorted(self.cnt.keys())
        sems = {n: stack.enter_context(nc.semaphore("s_" + n)) for n in names}
        block = stack.enter_context(nc.Block())
        emap = {"pe": block.tensor, "act": block.scalar, "dve": block.vector,
                "pool": block.gpsimd, "sp": block.sync}
        for e in self.ENG:
            ops = self.ops[e]

            def body(eng, ops=ops):
                for waits, fn, incsem, incval in ops:
                    for src, val in waits:
                        eng.wait_ge(sems[src], val)
                    if fn is None:
                        continue
                    ins = fn(eng)
                    if incsem is not None:
                        ins.then_inc(sems[incsem], incval)
            emap[e](body)


DBG = {"stage": 0}


def build_program(NT):
    nc = bass.Bass("TRN2", target_bir_lowering=False)
    LTOT = NMETA + NT * T
    x_d = nc.dram_tensor("x", [NT * T, D], F32, kind="ExternalInput").ap()
    meta_d = nc.dram_tensor("meta", [NMETA, D], F32, kind="ExternalInput").ap()
    lnbc_d = nc.dram_tensor("lnbc", [6, 128, D], F32, kind="ExternalInput").ap()
    consts_d = nc.dram_tensor("consts", [128, NCONST], F32, kind="ExternalInput").ap()
    ident_d = nc.dram_tensor("ident", [128, 128], F32, kind="ExternalInput").ap()
    rope_d = nc.dram_tensor("rope", [2, 32, LTOT], F32, kind="ExternalInput").ap()
    poolw_d = nc.dram_tensor("poolw", [128, 512], F32, kind="ExternalInput").ap()
    wts_d = nc.dram_tensor("wts", [NUNITS, 128, USZ], F32, kind="ExternalInput").ap()
    out_d = nc.dram_tensor("out", [NT * T, D], F32, kind="ExternalOutput").ap()
    kvk_d = nc.dram_tensor("kvk", [8, 128, NT * T], BF16).ap()
    kvv_d = nc.dram_tensor("kvv", [8, 128, NT * T], BF16).ap()

    P = Prog()
    st = ExitStack()
    with st:
        def sb(name, shape, dt):
            return st.enter_context(nc.sbuf_tensor("sb_" + name, shape, dt))

        lnbc = [sb(f"lnbc{i}", [128, D], F32) for i in range(6)]
        consts = sb("consts", [128, NCONST], F32)
        ident = sb("ident", [128, 128], F32)
        ones = sb("ones", [128, 128], F32)
        poolw = sb("poolw", [128, 512], BF16)
        kmeta = sb("kmeta", [128, 8 * 128], BF16)
        vmeta = sb("vmeta", [128, 1024], BF16)
        ahalo = sb("ahalo", [128, 2 * 2 * NFC * 2], F32)
        slots = [sb(f"slot{i}", [128, USZ], BF16) for i in range(NSLOT)]
        htm = [sb(f"htm{i}", [128, D], F32) for i in range(4)]
        xn = [sb(f"xn{i}", [128, D], F32) for i in range(4)]
        hT = sb("hT", [128, 8 * T], BF16)
        hTm = sb("hTm", [128, 8 * NMETA], BF16)
        arena = sb("arena", [128, 22 * T], BF16)
        arena2 = [sb(f"ar2_{i}", [128, 516], F32) for i in range(10)]
        vp = sb("vp", [128, 4 * 528], F32)
        ptmp = [sb(f"ptmp{i}", [128, 528], F32) for i in range(2)]
        pmin = sb("pmin", [128, 4 * T], BF16)
        pm = sb("pm", [128, 4 * T], BF16)
        cn = sb("cn", [128, 5 * T], BF16)
        ropet = sb("ropet", [32, 2 * T], F32)
        kcur = sb("kcur", [128, 8 * T], BF16)
        vcur = sb("vcur", [128, 4 * 1024], BF16)
        pt3 = sb("pt3", [128, T], BF16)
        rec = sb("rec", [64, T], F32)
        sg = [sb(f"sg{i}", [128, T], F32) for i in range(2)]
        stat = sb("stat", [128, 32], F32)
        psum = [st.enter_context(nc.psum_tensor(f"ps{i}", [128, 512], F32)) for i in range(8)]

        B_lnbc = Buf("lnbc"); B_consts = Buf("consts"); B_ident = Buf("ident"); B_ones = Buf("ones")
        B_poolw = Buf("poolw"); B_kmeta = Buf("kmeta"); B_vmeta = Buf("vmeta")
        B_ahalo = [Buf(f"ahalo{i}") for i in range(4 * NFC)]
        B_slot = [Buf(f"slot{i}") for i in range(NSLOT)]
        B_htm = [Buf(f"htm{i}") for i in range(4)]
        B_xn = [Buf(f"xn{i}") for i in range(4)]
        B_hT = [Buf(f"hT{i}") for i in range(8)]
        B_hTm = Buf("hTm")
        B_ar = [Buf(f"ar{i}") for i in range(22)]
        B_ar2 = [Buf(f"ar2_{i}") for i in range(10)]
        B_vp = [Buf(f"vp{i}") for i in range(4)]
        B_ptmp = [Buf("ptmp0"), Buf("ptmp1")]
        B_pmin = [Buf(f"pmin{i}") for i in range(4)]
        B_pm = [Buf(f"pm{i}") for i in range(4)]
        B_cn = [Buf(f"cn{i}") for i in range(5)]
        B_ropet = Buf("ropet")
        B_kcur = [Buf(f"kcur{i}") for i in range(8)]
        B_vcur = [Buf(f"vcur{i}") for i in range(4)]
        B_pt3 = Buf("pt3"); B_rec = Buf("rec")
        B_sg = [Buf(f"sg{i}") for i in range(2)]
        B_stat = Buf("stat")
        sqs = arena[:, 8 * T:18 * T].bitcast(F32)
        B_sqs = [[B_ar[8 + 2 * i], B_ar[9 + 2 * i]] for i in range(5)]
        B_ps = [Buf(f"ps{i}") for i in range(8)]
        B_kvk = [Buf(f"kvk{i}") for i in range(NT)]
        B_kvv = [Buf(f"kvv{i}") for i in range(NT)]

        cst = lambda c0, n=1: consts[:, c0:c0 + n]

        for i in range(6):
            P.dma("sp", lambda e, i=i: e.dma_start(out=lnbc[i][:], in_=lnbc_d[i]), "cl", writes=[B_lnbc])
        P.dma("sp", lambda e: e.dma_start(out=consts[:], in_=consts_d), "cc", writes=[B_consts])
        P.dma("sp", lambda e: e.dma_start(out=ident[:], in_=ident_d), "ci", writes=[B_ident])
        P.dma("pool", lambda e: e.dma_start(out=poolw[:], in_=poolw_d), "c1", writes=[B_poolw])
        P.emit("dve", lambda e: e.memset(ones[:], 1.0), writes=[B_ones])
        P.emit("dve", lambda e: e.memset(kmeta[:], 0.0), writes=[B_kmeta])
        P.emit("dve", lambda e: e.memset(vmeta[:], 0.0), writes=[B_vmeta])
        P.emit("dve", lambda e: e.memset(vp[:], 0.0), writes=B_vp)
        P.emit("dve", lambda e: e.memset(ahalo[:], 0.0), writes=B_ahalo)
        P.emit("dve", lambda e: e.memset(kcur[:], 0.0), writes=B_kcur)
        vcur_v = vcur[:].rearrange("p (h s c) -> p h s c", h=8, s=4)
        for s in range(4):
            P.emit("dve", lambda e, s=s: e.memset(vcur_v[:, :, s, 64:128], 1.0), writes=[B_vcur[s]])
        vmeta_v = vmeta[:].rearrange("p (h c) -> p h c", h=8)
        P.emit("dve", lambda e: e.memset(vmeta_v[0:NMETA, :, 64:128], 1.0), writes=[B_vmeta])

        class Stream:
            def __init__(self):
                self.units = []
                self.uslot = {}
                self.next_dma = 0
                self.head = 0
                self.free = list(range(NSLOT))

            def plan(self, lst):
                self.units.extend(lst)

            def _issue(self, k, slot):
                u = self.units[k]
                sl = slots[slot]
                sem = f"sl{slot}"
                if u[0] == "w":
                    _, idx, ncols = u
                    P.dma("pool", lambda e: e.dma_start(out=sl[:, 0:ncols], in_=wts_d[idx, :, 0:ncols]),
                          sem, writes=[B_slot[slot]])
                else:
                    _, h, jp0, n = u
                    rd = [B_kvk[jp] for jp in range(jp0, jp0 + n)] + [B_kvv[jp] for jp in range(jp0, jp0 + n)]
                    semk = f"sk{slot}"
                    P.dma("sp", lambda e: e.dma_start(out=sl[0:96, 0:n * T], in_=kvk_d[h, 0:96, jp0 * T:(jp0 + n) * T]),
                          semk, reads=rd, writes=[B_slot[slot]])
                    P.dma("sp", lambda e: e.dma_start(out=sl[:, 2048:2048 + n * T], in_=kvv_d[h, :, jp0 * T:(jp0 + n) * T]),
                          semk, reads=rd, writes=[B_slot[slot]])

            def pump(self):
                while self.free and self.next_dma < len(self.units):
                    slot = self.free.pop(0)
                    self.uslot[self.next_dma] = slot
                    self._issue(self.next_dma, slot)
                    self.next_dma += 1

            def acquire(self, *key):
                k = self.head
                assert tuple(self.units[k][:len(key)]) == tuple(key), (self.units[k], key)
                self.pump()
                assert k in self.uslot, "stream deadlock: no free slot"
                self.head += 1
                slot = self.uslot[k]
                return slots[slot], B_slot[slot], slot

            def release(self, slot):
                self.free.append(slot)
                self.pump()

        S = Stream()

        def kv_groups(j):
            g = []
            jp0 = 0
            while jp0 < j:
                n = min(4, j - jp0)
                g.append((jp0, n))
                jp0 += n
            return g

        def tile_units(j):
            u = [("w", U_INB, 4096), ("w", U_INC, 2048), ("w", U_INA, 4096),
                 ("w", U_UQ, 3 * 1152), ("w", U_UKV, 2048)]
            for h in range(8):
                for (jp0, n) in kv_groups(max(j, 0)):
                    u.append(("kv", h, jp0, n))
            u += [("w", U_PP, 4096), ("w", U_PMLA, 4096)]
            u += [("w", U_G0 + i, 4096) for i in range(4)]
            u += [("w", U_WO + i, 4096) for i in range(2)]
            if j >= 0:
                u += [("w", U_UP + i, 4096) for i in range(11)]
                u += [("w", U_DN + i, 4096) for i in range(6)]
            return u

        for j in range(-1, NT):
            S.plan(tile_units(j))

        bank_state = {"i": 0, "lo": 0, "hi": 8}

        def nbank():
            lo, hi = bank_state["lo"], bank_state["hi"]
            i = bank_state["i"]
            if i < lo or i >= hi:
                i = lo
            bank_state["i"] = i + 1 if i + 1 < hi else lo
            return i

        def mm(out, lhsT, rhs, start, stop, reads, writes, signal):
            P.emit("pe", lambda e: e.matmul(out, lhsT=lhsT, rhs=rhs, start=start, stop=stop),
                   reads=reads, writes=writes, signal=signal)

        evac_rr = {"i": 0}

        def copy_evac(out, in_, reads, writes, force=None):
            evac_rr["i"] ^= 1
            if force == "act" or (force is None and evac_rr["i"]):
                P.emit("act", lambda e: e.activation(out=out, in_=in_, func=AF.Copy), reads=reads, writes=writes)
            else:
                P.emit("dve", lambda e: e.tensor_copy(out=out, in_=in_), reads=reads, writes=writes)

        def ln_normalize(h, b, R):
            P.emit("dve", lambda e: e.bn_stats(out=stat[0:R, 0:6], in_=h[0:R, 0:512]), reads=[b], writes=[B_stat])
            P.emit("dve", lambda e: e.bn_stats(out=stat[0:R, 6:12], in_=h[0:R, 512:1024]), reads=[b], writes=[B_stat])
            P.emit("dve", lambda e: e.bn_aggr(out=stat[0:R, 12:14], in_=stat[0:R, 0:12]), reads=[B_stat], writes=[B_stat])
            P.emit("act", lambda e: e.activation(out=stat[0:R, 14:15], in_=stat[0:R, 13:14], func=AF.Ln,
                                                 bias=consts[0:R, C_EPS:C_EPS + 1], scale=1.0),
                   reads=[B_stat, B_consts], writes=[B_stat])
            P.emit("act", lambda e: e.activation(out=stat[0:R, 15:16], in_=stat[0:R, 14:15], func=AF.Exp, scale=-0.5),
                   reads=[B_stat], writes=[B_stat])
            P.emit("dve", lambda e: e.tensor_scalar(out=stat[0:R, 16:17], in0=stat[0:R, 12:13], scalar1=stat[0:R, 15:16],
                                                    scalar2=-1.0, op0=ALU.mult, op1=ALU.mult),
                   reads=[B_stat], writes=[B_stat])
            P.emit("act", lambda e: e.activation(out=h[0:R, :], in_=h[0:R, :], func=AF.Identity,
                                                 bias=stat[0:R, 16:17], scale=stat[0:R, 15:16]),
                   reads=[b, B_stat], writes=[b])

        def ln_affine(h, b, R, gi, aff, extra=()):
            P.emit(aff, lambda e: e.tensor_tensor(out=h[0:R, :], in0=h[0:R, :], in1=lnbc[gi][0:R, :], op=ALU.mult),
                   reads=[b, B_lnbc] + list(extra), writes=[b])
            P.emit(aff, lambda e: e.tensor_tensor(out=h[0:R, :], in0=h[0:R, :], in1=lnbc[gi + 1][0:R, :], op=ALU.add),
                   reads=[b, B_lnbc], writes=[b])

        def layer_norm(h, b, R, gi, aff="dve"):
            ln_normalize(h, b, R)
            ln_affine(h, b, R, gi, aff)

        def transpose_to_hT(src, bsrc, NS, R, Tt, lt, to_meta=False):
            for c in range(8):
                bk = nbank()
                for s in range(NS):
                    P.emit("pe", lambda e, s=s, c=c, bk=bk: e.transpose(psum[bk][:, s * 128:s * 128 + R],
                                                                       src[s][0:R, c * 128:(c + 1) * 128], ident[0:R, 0:R]),
                           reads=[bsrc[s], B_ident], writes=[B_ps[bk]], signal=(s == NS - 1))
                gcol = consts[:, C_LT + lt * 8 + c:C_LT + lt * 8 + c + 1]
                bcol = consts[:, C_LT + (lt + 1) * 8 + c:C_LT + (lt + 1) * 8 + c + 1]
                if to_meta:
                    dsto, bdst = hTm[:, c * NMETA:(c + 1) * NMETA], B_hTm
                else:
                    dsto, bdst = hT[:, c * T:c * T + Tt], B_hT[c]
                if c % 2 == 0:
                    P.emit("act", lambda e, dsto=dsto, bk=bk, gcol=gcol, bcol=bcol: e.activation(
                        out=dsto, in_=psum[bk][:, 0:Tt], func=AF.Identity, bias=bcol, scale=gcol),
                        reads=[B_ps[bk], B_consts], writes=[bdst])
                else:
                    P.emit("dve", lambda e, dsto=dsto, bk=bk, gcol=gcol, bcol=bcol: e.tensor_scalar(
                        out=dsto, in0=psum[bk][:, 0:Tt], scalar1=gcol, scalar2=bcol, op0=ALU.mult, op1=ALU.add),
                        reads=[B_ps[bk], B_consts], writes=[bdst])

        out_events = []

        def dump_tm():
            for s in range(4):
                ev = P.dma("sp", lambda e, s=s: e.dma_start(out=out_d[s * 128:(s + 1) * 128, :], in_=htm[s][:, :]), f"out{s}", reads=[B_htm[s]])
                out_events.append(ev)

        def dump_fm(ap, bufs, f0):
            for tc in range(8):
                ev = P.dma("pool", lambda e, tc=tc: e.dma_start(out=out_d[tc * 64:(tc + 1) * 64, f0:f0 + 128].rearrange("t f -> f t"), in_=ap[:, tc * 64:(tc + 1) * 64], allow_slow_non_contiguous=True), "outd", reads=bufs)
                out_events.append(ev)
        def tp(j):
            is_meta = j < 0
            return dict(is_meta=is_meta, Tt=NMETA if is_meta else T, NS=1 if is_meta else 4,
                        R=NMETA if is_meta else 128, pos0=0 if is_meta else NMETA + j * T)

        ra, rb = arena2[8], arena2[9]

        def phase_load_ln_in(j):
            c = tp(j)
            Tt, NS, R, pos0 = c["Tt"], c["NS"], c["R"], c["pos0"]
            if c["is_meta"]:
                P.dma("sp", lambda e: e.dma_start(out=xn[0][0:NMETA, :], in_=meta_d), "xin0", writes=[B_xn[0]])
            else:
                for s in range(NS):
                    P.dma("sp", lambda e, s=s: e.dma_start(out=xn[s][:, :], in_=x_d[j * T + s * 128:j * T + (s + 1) * 128, :]),
                          f"xin{s}", writes=[B_xn[s]])
            P.dma("sp", lambda e: e.dma_start(out=ropet[:, 0:Tt], in_=rope_d[0, :, pos0:pos0 + Tt]), "rope", writes=[B_ropet])
            P.dma("sp", lambda e: e.dma_start(out=ropet[:, T:T + Tt], in_=rope_d[1, :, pos0:pos0 + Tt]), "rope", writes=[B_ropet])
            for s in range(NS):
                ln_normalize(xn[s], B_xn[s], R)

        def phase_front(j):
            c = tp(j)
            is_meta, Tt, NS, R = c["is_meta"], c["Tt"], c["NS"], c["R"]
            bank_state["lo"], bank_state["hi"] = 0, 8

            def hTc(kc):
                return hT[:, kc * T:kc * T + Tt]

            cc = ropet[:, 0:Tt]
            ss = ropet[:, T:T + Tt]

            def rms_part1(wq, bq, ncols, chunks, ar0, sq0):
                for ci in range(chunks):
                    bk = nbank()
                    for kc in range(8):
                        mm(psum[bk][:, 0:Tt], wq[:, kc * ncols + ci * 128:kc * ncols + (ci + 1) * 128], hTc(kc),
                           kc == 0, kc == 7, [bq, B_hT[kc]], [B_ps[bk]], kc == 7)
                    a = arena2[ar0 + ci]
                    P.emit("act", lambda e, a=a, bk=bk: e.activation(out=a[:, 0:Tt], in_=psum[bk][:, 0:Tt], func=AF.Copy),
                           reads=[B_ps[bk]], writes=[B_ar2[ar0 + ci]])
                    P.emit("act", lambda e, ci=ci, bk=bk: e.activation(out=sqs[:, (sq0 + ci) * T:(sq0 + ci) * T + Tt], in_=psum[bk][:, 0:Tt], func=AF.Square),
                           reads=[B_ps[bk]], writes=B_sqs[sq0 + ci])

            def rms_part2(chunks, gcol, nfeat, ar0, cn0, rstd_i, sq0):
                bk2 = nbank()
                for ci in range(chunks):
                    mm(psum[bk2][:, 0:Tt], ones[:, :], sqs[:, (sq0 + ci) * T:(sq0 + ci) * T + Tt], ci == 0, ci == chunks - 1,
                       [B_ones] + B_sqs[sq0 + ci], [B_ps[bk2]], ci == chunks - 1)
                rs = arena2[rstd_i]
                P.emit("act", lambda e: e.activation(out=rs[:, 0:Tt], in_=psum[bk2][:, 0:Tt], func=AF.Ln,
                                                     bias=consts[:, C_EPS:C_EPS + 1], scale=1.0 / nfeat),
                       reads=[B_ps[bk2], B_consts], writes=[B_ar2[rstd_i]])
                P.emit("act", lambda e: e.activation(out=rs[:, 0:Tt], in_=rs[:, 0:Tt], func=AF.Exp, scale=-0.5),
                       reads=[B_ar2[rstd_i]], writes=[B_ar2[rstd_i]])
                for ci in range(chunks):
                    a = arena2[ar0 + ci]
                    P.emit("dve", lambda e, a=a, ci=ci: e.scalar_tensor_tensor(
                        out=cn[:, (cn0 + ci) * T:(cn0 + ci) * T + Tt], in0=a[:, 0:Tt], scalar=consts[:, gcol + ci:gcol + ci + 1],
                        in1=rs[:, 0:Tt], op0=ALU.mult, op1=ALU.mult),
                        reads=[B_ar2[ar0 + ci], B_ar2[rstd_i], B_consts], writes=[B_cn[cn0 + ci]])

            wB, bwB, slB = S.acquire("w", U_INB)
            rms_part1(wB, bwB, 512, 3, 0, 0)
            bkr = nbank()
            for kc in range(8):
                mm(psum[bkr][:, 0:Tt], wB[:, kc * 512 + 384:kc * 512 + 512], hTc(kc), kc == 0, kc == 7,
                   [bwB, B_hT[kc]], [B_ps[bkr]], kc == 7)
            S.release(slB)
            P.emit("dve", lambda e: e.tensor_tensor(out=ra[0:32, 0:Tt], in0=psum[bkr][0:32, 0:Tt], in1=cc, op=ALU.mult),
                   reads=[B_ps[bkr], B_ropet], writes=[B_ar2[8]])
            P.emit("dve", lambda e: e.tensor_tensor(out=rb[0:32, 0:Tt], in0=psum[bkr][32:64, 0:Tt], in1=ss, op=ALU.mult),
                   reads=[B_ps[bkr], B_ropet], writes=[B_ar2[9]])
            P.emit("dve", lambda e: e.tensor_tensor(out=ra[0:32, 0:Tt], in0=ra[0:32, 0:Tt], in1=rb[0:32, 0:Tt], op=ALU.add),
                   reads=[B_ar2[8], B_ar2[9]], writes=[B_ar2[8]])
            for h in range(8):
                if is_meta:
                    dst, bd = kmeta[64:96, h * 128:h * 128 + NMETA], B_kmeta
                else:
                    dst, bd = kcur[64:96, h * T:(h + 1) * T], B_kcur[h]
                copy_evac(dst, ra[0:32, 0:Tt], [B_ar2[8]], [bd], force="act")
            wC, bwC, slC = S.acquire("w", U_INC)
            rms_part1(wC, bwC, 256, 2, 3, 3)
            S.release(slC)

            w, bw, sl = S.acquire("w", U_INA)
            for g in range(4):
                bk = nbank()
                for kc in range(8):
                    mm(psum[bk][:, 0:Tt], w[:, kc * 512 + g * 128:kc * 512 + (g + 1) * 128], hTc(kc),
                       kc == 0, kc == 7, [bw, B_hT[kc]], [B_ps[bk]], kc == 7)
                P.emit("act", lambda e, g=g, bk=bk: e.activation(out=vp[:, g * 528 + 16:g * 528 + 16 + Tt],
                                                               in_=psum[bk][:, 0:Tt], func=AF.Copy),
                       reads=[B_ps[bk]], writes=[B_vp[g]])
            S.release(sl)

            rms_part2(3, C_QG, 384.0, 0, 0, 6, 0)
            rms_part2(2, C_KG, 256.0, 3, 3, 7, 3)
            wq, bwq, slq = S.acquire("w", U_UQ)
            NQ = 1152
            swb = []
            bank_state["lo"], bank_state["hi"] = 0, 3
            for gq in range(3):
                bk = nbank()
                swb.append(bk)
                for kc in range(3):
                    mm(psum[bk][:, 0:Tt], wq[:, kc * NQ + 768 + gq * 128:kc * NQ + 768 + (gq + 1) * 128],
                       cn[:, kc * T:kc * T + Tt], kc == 0, kc == 2, [bwq, B_cn[kc]], [B_ps[bk]], kc == 2)
            bank_state["lo"], bank_state["hi"] = 3, 8
            for h in range(8):
                bk = nbank()
                assert bk not in swb
                for kc in range(3):
                    mm(psum[bk][0:96, 0:Tt], wq[:, kc * NQ + h * 96:kc * NQ + (h + 1) * 96],
                       cn[:, kc * T:kc * T + Tt], kc == 0, kc == 2, [bwq, B_cn[kc]], [B_ps[bk]], kc == 2)
                P.emit("act", lambda e, h=h, bk=bk: e.activation(out=arena[0:64, h * T:h * T + Tt], in_=psum[bk][0:64, 0:Tt], func=AF.Copy),
                       reads=[B_ps[bk]], writes=[B_ar[h]])
                sbk = swb[h // 3]
                i3 = h % 3
                P.emit("dve", lambda e, bk=bk: e.tensor_tensor(out=ra[0:32, 0:Tt], in0=psum[bk][64:96, 0:Tt], in1=cc, op=ALU.mult),
                       reads=[B_ps[bk], B_ropet], writes=[B_ar2[8]])
                P.emit("dve", lambda e, sbk=sbk, i3=i3: e.tensor_tensor(out=rb[0:32, 0:Tt], in0=psum[sbk][i3 * 32:(i3 + 1) * 32, 0:Tt], in1=ss, op=ALU.mult),
                       reads=[B_ps[sbk], B_ropet], writes=[B_ar2[9]])
                P.emit("dve", lambda e, h=h: e.tensor_tensor(out=arena[64:96, h * T:h * T + Tt], in0=ra[0:32, 0:Tt], in1=rb[0:32, 0:Tt], op=ALU.add),
                       reads=[B_ar2[8], B_ar2[9]], writes=[B_ar[h]])
            S.release(slq)
            bank_state["lo"], bank_state["hi"] = 0, 8
            wkv, bwkv, slkv = S.acquire("w", U_UKV)
            for hp in range(4):
                bk = nbank()
                for kc in range(2):
                    mm(psum[bk][:, 0:Tt], wkv[:, kc * 1024 + hp * 128:kc * 1024 + (hp + 1) * 128],
                       cn[:, (3 + kc) * T:(3 + kc) * T + Tt], kc == 0, kc == 1, [bwkv, B_cn[3 + kc]], [B_ps[bk]], kc == 1)
                for hh in range(2):
                    h = 2 * hp + hh
                    if is_meta:
                        dst, bd = kmeta[0:64, h * 128:h * 128 + NMETA], B_kmeta
                    else:
                        dst, bd = kcur[0:64, h * T:(h + 1) * T], B_kcur[h]
                    copy_evac(dst, psum[bk][hh * 64:(hh + 1) * 64, 0:Tt], [B_ps[bk]], [bd], force="act")
            for s in range(NS):
                bk = nbank()
                for kc in range(2):
                    mm(psum[bk][0:R, :], cn[:, (3 + kc) * T + s * 128:(3 + kc) * T + s * 128 + R],
                       wkv[:, kc * 1024 + 512:kc * 1024 + 1024], kc == 0, kc == 1, [bwkv, B_cn[3 + kc]], [B_ps[bk]], kc == 1)
                src = psum[bk][0:R, :].rearrange("p (h c) -> p h c", h=8)
                if is_meta:
                    dst, bd = vmeta_v[0:R, :, 0:64], B_vmeta
                else:
                    dst, bd = vcur_v[:, :, s, 0:64], B_vcur[s]
                copy_evac(dst, src, [B_ps[bk]], [bd], force="act")
            S.release(slkv)
            if not is_meta and j < NT - 1:
                P.dma("sp", lambda e: e.dma_start(out=kvk_d[:, 0:96, j * T:(j + 1) * T].rearrange("h p t -> p h t"),
                                                  in_=kcur[0:96, :].rearrange("p (h t) -> p h t", h=8)),
                      "kvstk", reads=B_kcur, writes=[B_kvk[j]])
                P.dma("sp", lambda e: e.dma_start(out=kvv_d[:, :, j * T:(j + 1) * T].rearrange("h p t -> p h t"),
                                                  in_=vcur[:, :].rearrange("p (h t) -> p h t", h=8)),
                      "kvstv", reads=B_vcur, writes=[B_kvv[j]])

        def phase_pool(j):
            c = tp(j)
            is_meta, Tt = c["is_meta"], c["Tt"]
            wins = (2, 4, 8, 16)
            for g in range(4):
                v = vp[:, g * 528:(g + 1) * 528]
                bv = B_vp[g]
                n = 16 + Tt
                src, bsrc = v, bv
                lo = 0
                step = 1
                ti = 0
                while step < wins[g]:
                    dst, bdst = ptmp[ti], B_ptmp[ti]
                    nlo = lo + step
                    P.emit("dve", lambda e, dst=dst, src=src, nlo=nlo, step=step, n=n: e.tensor_tensor(
                        out=dst[:, nlo:n], in0=src[:, nlo:n], in1=src[:, nlo - step:n - step], op=ALU.add),
                        reads=[bsrc], writes=[bdst])
                    src, bsrc, lo = dst, bdst, nlo
                    step *= 2
                    ti ^= 1
                if is_meta:
                    P.emit("dve", lambda e, src=src, g=g: e.tensor_tensor(
                        out=src[:, 16:16 + Tt], in0=src[:, 16:16 + Tt], in1=consts[:, C_IC + g * 16:C_IC + (g + 1) * 16], op=ALU.mult),
                        reads=[bsrc, B_consts], writes=[bsrc])
                    P.emit("dve", lambda e, src=src, v=v, g=g: e.tensor_tensor(
                        out=pmin[:, g * T:g * T + Tt], in0=src[:, 16:16 + Tt], in1=v[:, 16:16 + Tt], op=ALU.subtract),
                        reads=[bsrc, bv], writes=[B_pmin[g]])
                else:
                    P.emit("dve", lambda e, src=src, v=v, g=g: e.scalar_tensor_tensor(
                        out=pmin[:, g * T:g * T + Tt], in0=src[:, 16:16 + Tt], scalar=1.0 / wins[g], in1=v[:, 16:16 + Tt],
                        op0=ALU.mult, op1=ALU.subtract),
                        reads=[bsrc, bv], writes=[B_pmin[g]])
                P.emit("act", lambda e, v=v: e.activation(out=v[:, 0:16], in_=v[:, Tt:Tt + 16], func=AF.Copy), reads=[bv], writes=[bv])

        def phase_attn(j, hook=None):
            c = tp(j)
            is_meta, Tt = c["is_meta"], c["Tt"]
            bank_state["lo"], bank_state["hi"] = 2, 8
            PT = [(arena[:, 20 * T:21 * T], B_ar[20]), (arena[:, 21 * T:22 * T], B_ar[21]), (pt3[:, :], B_pt3)]
            pti = {"i": 0}
            LA = 2
            pend = []

            def attn_item(h, K, bK, V, bV, n0, first, last, obk, rel):
                sbk = nbank()
                q = arena[0:96, h * T + n0:h * T + Tt]
                mm(psum[sbk][:, n0:Tt], K, q, True, True, [bK, B_ar[h]], [B_ps[sbk]], True)
                pt, bpt = PT[pti["i"]]
                pti["i"] = (pti["i"] + 1) % 3

                def tail():
                    P.emit("act", lambda e: e.activation(out=pt[:, n0:Tt], in_=psum[sbk][:, n0:Tt], func=AF.Exp, scale=ATTN_SCALE),
                           reads=[B_ps[sbk]], writes=[bpt])
                    if rel:
                        P.emit("act", lambda e: e.memzero(pt[64:128, n0:n0 + 64]), writes=[bpt])
                    mm(psum[obk][:, n0:Tt], V, pt[:, n0:Tt], first, last, [bV, bpt], [B_ps[obk]], True)
                    if last:
                        P.emit("dve", lambda e: e.reciprocal(out=rec[0:64, 0:Tt], in_=psum[obk][64:128, 0:Tt]),
                               reads=[B_ps[obk]], writes=[B_rec])
                        hp, hh = h // 2, h % 2
                        P.emit("dve", lambda e: e.tensor_tensor(out=arena[hh * 64:(hh + 1) * 64, (16 + hp) * T:(16 + hp) * T + Tt],
                                                                in0=psum[obk][0:64, 0:Tt], in1=rec[0:64, 0:Tt], op=ALU.mult),
                               reads=[B_ps[obk], B_rec], writes=[B_ar[16 + hp]])
                pend.append(tail)
                if len(pend) > LA:
                    pend.pop(0)()

            for h in range(8):
                if hook is not None:
                    hook(h)
                obk = h % 2
                items = [(kmeta[0:96, h * 128:(h + 1) * 128], B_kmeta, vmeta[:, h * 128:(h + 1) * 128], B_vmeta, 0, False, None)]
                for (jp0, n) in kv_groups(max(j, 0)):
                    ks, bks, slk = S.acquire("kv", h, jp0, n)
                    for bi in range(4 * n):
                        items.append((ks[0:96, bi * 128:(bi + 1) * 128], bks, ks[:, 2048 + bi * 128:2048 + (bi + 1) * 128], bks, 0,
                                      False, slk if bi == 4 * n - 1 else None))
                if not is_meta:
                    for kb in range(4):
                        items.append((kcur[0:96, h * T + kb * 128:h * T + (kb + 1) * 128], B_kcur[h],
                                      vcur[:, (h * 4 + kb) * 128:(h * 4 + kb + 1) * 128], B_vcur[kb], kb * 128, True, None))
                for ii, (K, bK, V, bV, n0, rel, relslot) in enumerate(items):
                    attn_item(h, K, bK, V, bV, n0, ii == 0, ii == len(items) - 1, obk, rel)
                    if relslot is not None:
                        while pend:
                            pend.pop(0)()
                        S.release(relslot)
            while pend:
                pend.pop(0)()
            bank_state["lo"], bank_state["hi"] = 0, 8

        def phase_merge_wout_ln1(j):
            c = tp(j)
            Tt, NS, R = c["Tt"], c["NS"], c["R"]

            def hTc(kc):
                return hT[:, kc * T:kc * T + Tt]

            for g in range(4):
                bk = nbank()
                mm(psum[bk][:, 0:Tt], poolw[:, g * 128:(g + 1) * 128], pmin[:, g * T:g * T + Tt], True, True,
                   [B_poolw, B_pmin[g]], [B_ps[bk]], True)
                P.emit("act", lambda e, g=g, bk=bk: e.activation(out=pm[:, g * T:g * T + Tt], in_=psum[bk][:, 0:Tt], func=AF.Identity,
                                                               scale=consts[:, C_PS + g:C_PS + g + 1]),
                       reads=[B_ps[bk], B_consts], writes=[B_pm[g]])
            wpp, bwpp, slpp = S.acquire("w", U_PP)
            wpm, bwpm, slpm = S.acquire("w", U_PMLA)
            for u in range(4):
                wg, bwg, slg = S.acquire("w", U_G0 + u)
                for mi in range(2):
                    m = 2 * u + mi
                    bky = nbank()
                    for g in range(4):
                        mm(psum[bky][:, 0:Tt], wpp[:, g * 1024 + m * 128:g * 1024 + (m + 1) * 128], pm[:, g * T:g * T + Tt],
                           g == 0, g == 3, [bwpp, B_pm[g]], [B_ps[bky]], g == 3)
                    bkg0 = nbank()
                    for kc in range(8):
                        mm(psum[bkg0][:, 0:Tt], wg[:, kc * 512 + mi * 256:kc * 512 + mi * 256 + 128], hTc(kc),
                           kc == 0, kc == 7, [bwg, B_hT[kc]], [B_ps[bkg0]], kc == 7)
                    bkm = nbank()
                    for hp in range(4):
                        mm(psum[bkm][:, 0:Tt], wpm[:, hp * 1024 + m * 128:hp * 1024 + (m + 1) * 128],
                           arena[:, (16 + hp) * T:(16 + hp) * T + Tt], hp == 0, hp == 3, [bwpm, B_ar[16 + hp]], [B_ps[bkm]], hp == 3)
                    bkg1 = nbank()
                    for kc in range(8):
                        mm(psum[bkg1][:, 0:Tt], wg[:, kc * 512 + mi * 256 + 128:kc * 512 + mi * 256 + 256], hTc(kc),
                           kc == 0, kc == 7, [bwg, B_hT[kc]], [B_ps[bkg1]], kc == 7)
                    P.emit("act", lambda e, m=m, bkg0=bkg0: e.activation(out=sg[0][:, 0:Tt], in_=psum[bkg0][:, 0:Tt], func=AF.Sigmoid,
                                                                       bias=consts[:, C_BG + m:C_BG + m + 1], scale=1.0),
                           reads=[B_ps[bkg0], B_consts], writes=[B_sg[0]])
                    P.emit("act", lambda e, m=m, bkg1=bkg1: e.activation(out=sg[1][:, 0:Tt], in_=psum[bkg1][:, 0:Tt], func=AF.Sigmoid,
                                                                       bias=consts[:, C_BG + 8 + m:C_BG + 8 + m + 1], scale=1.0),
                           reads=[B_ps[bkg1], B_consts], writes=[B_sg[1]])
                    P.emit("dve", lambda e, bky=bky: e.tensor_tensor(out=sg[0][:, 0:Tt], in0=sg[0][:, 0:Tt], in1=psum[bky][:, 0:Tt], op=ALU.mult),
                           reads=[B_sg[0], B_ps[bky]], writes=[B_sg[0]])
                    P.emit("dve", lambda e, bkm=bkm: e.tensor_tensor(out=sg[1][:, 0:Tt], in0=sg[1][:, 0:Tt], in1=psum[bkm][:, 0:Tt], op=ALU.mult),
                           reads=[B_sg[1], B_ps[bkm]], writes=[B_sg[1]])
                    P.emit("dve", lambda e, m=m: e.tensor_tensor(out=arena[:, (8 + m) * T:(8 + m) * T + Tt], in0=sg[0][:, 0:Tt], in1=sg[1][:, 0:Tt], op=ALU.add),
                           reads=[B_sg[0], B_sg[1]], writes=[B_ar[8 + m]])
                S.release(slg)
            S.release(slpp)
            S.release(slpm)
            wo0, bwo0, slo0 = S.acquire("w", U_WO)
            wo1, bwo1, slo1 = S.acquire("w", U_WO + 1)
            for s in range(NS):
                for half, (wo, bwo) in enumerate(((wo0, bwo0), (wo1, bwo1))):
                    bk = nbank()
                    for kc in range(8):
                        mm(psum[bk][0:R, :], arena[:, (8 + kc) * T + s * 128:(8 + kc) * T + s * 128 + R], wo[:, kc * 512:(kc + 1) * 512],
                           kc == 0, kc == 7, [bwo, B_ar[8 + kc]], [B_ps[bk]], kc == 7)
                    P.emit("dve", lambda e, s=s, half=half, bk=bk: e.scalar_tensor_tensor(
                        out=htm[s][0:R, half * 512:(half + 1) * 512], in0=xn[s][0:R, half * 512:(half + 1) * 512], scalar=ALPHA,
                        in1=psum[bk][0:R, :], op0=ALU.mult, op1=ALU.add),
                        reads=[B_xn[s], B_ps[bk]], writes=[B_htm[s]])
                ln_normalize(htm[s], B_htm[s], R)
            S.release(slo0)
            S.release(slo1)

        def phase_ffn_up(j):
            c = tp(j)
            is_meta, Tt = c["is_meta"], c["Tt"]
            rpar = j % 2 if j >= 0 else 0
            wpar = (j + 1) % 2

            def hTc(kc):
                return hT[:, kc * T:kc * T + Tt]

            dbl = 0
            for u in range(11):
                wu, bwu, slu = S.acquire("w", U_UP + u)
                for mi in range(2):
                    jc = 2 * u + mi
                    ab = [arena2[0 + dbl], arena2[2 + dbl]]
                    bab = [B_ar2[0 + dbl], B_ar2[2 + dbl]]
                    cb_ = [arena2[4 + dbl], arena2[6 + dbl]]
                    bcb = [B_ar2[4 + dbl], B_ar2[6 + dbl]]
                    sgb, bsgb = arena2[8 + dbl], B_ar2[8 + dbl]
                    acts, dves = [[], []], [[], []]
                    for br in range(2):
                        ch = br * NFC + jc
                        col0 = br * 256 + mi * 128
                        hold = ahalo[:, (rpar * 44 + ch) * 2:(rpar * 44 + ch) * 2 + 2]
                        hnew = ahalo[:, (wpar * 44 + ch) * 2:(wpar * 44 + ch) * 2 + 2]
                        if j == 0:
                            bkm = nbank()
                            for kc in range(8):
                                mm(psum[bkm][:, 0:NMETA], wu[:, kc * 512 + col0:kc * 512 + col0 + 128], hTm[:, kc * NMETA:(kc + 1) * NMETA],
                                   kc == 0, kc == 7, [bwu, B_hTm], [B_ps[bkm]], kc == 7)
                            P.emit("act", lambda e, hold=hold, bkm=bkm: e.activation(out=hold, in_=psum[bkm][:, NMETA - 2:NMETA], func=AF.Copy),
                                   reads=[B_ps[bkm]], writes=[B_ahalo[rpar * 44 + ch]])
                        bk = nbank()
                        for kc in range(8):
                            mm(psum[bk][:, 0:Tt], wu[:, kc * 512 + col0:kc * 512 + col0 + 128], hTc(kc),
                               kc == 0, kc == 7, [bwu, B_hT[kc]], [B_ps[bk]], kc == 7)
                        a, ba = ab[br], bab[br]
                        cc_, bc = cb_[br], bcb[br]
                        A, Dv = acts[br], dves[br]
                        A.append(lambda hnew=hnew, bk=bk, ch=ch: P.emit("act", lambda e: e.activation(out=hnew, in_=psum[bk][:, Tt - 2:Tt], func=AF.Copy),
                                                                        reads=[B_ps[bk]], writes=[B_ahalo[wpar * 44 + ch]]))
                        if is_meta:
                            continue
                        w0 = consts[:, C_CW + ch:C_CW + ch + 1]
                        w1 = consts[:, C_CW + 44 + ch:C_CW + 44 + ch + 1]
                        w2 = consts[:, C_CW + 88 + ch:C_CW + 88 + ch + 1]
                        bch = consts[:, C_CB + ch:C_CB + ch + 1]
                        rh = B_ahalo[rpar * 44 + ch]
                        if br == 0:
                            A.append(lambda a=a, hold=hold, ba=ba, rh=rh: P.emit("act", lambda e: e.activation(out=a[:, 0:2], in_=hold, func=AF.Copy),
                                                                             reads=[rh], writes=[ba]))
                            A.append(lambda a=a, bk=bk, ba=ba: P.emit("act", lambda e: e.activation(out=a[:, 2:2 + Tt], in_=psum[bk][:, 0:Tt], func=AF.Copy),
                                                                      reads=[B_ps[bk]], writes=[ba]))
                        A.append(lambda cc_=cc_, bk=bk, w2=w2, bch=bch, bc=bc: P.emit("act", lambda e: e.activation(
                            out=cc_[:, 0:Tt], in_=psum[bk][:, 0:Tt], func=AF.Identity, bias=bch, scale=w2),
                            reads=[B_ps[bk], B_consts], writes=[bc]))
                        if br == 0:
                            Dv.append(lambda a=a, cc_=cc_, w1=w1, ba=ba, bc=bc: P.emit("dve", lambda e: e.scalar_tensor_tensor(
                                out=cc_[:, 0:Tt], in0=a[:, 1:1 + Tt], scalar=w1, in1=cc_[:, 0:Tt],
                                op0=ALU.mult, op1=ALU.add), reads=[ba, bc, B_consts], writes=[bc]))
                            Dv.append(lambda a=a, cc_=cc_, w0=w0, ba=ba, bc=bc: P.emit("dve", lambda e: e.scalar_tensor_tensor(
                                out=cc_[:, 0:Tt], in0=a[:, 0:Tt], scalar=w0, in1=cc_[:, 0:Tt],
                                op0=ALU.mult, op1=ALU.add), reads=[ba, bc, B_consts], writes=[bc]))
                        else:
                            Dv.append(lambda cc_=cc_, bk=bk, w1=w1, bc=bc: P.emit("dve", lambda e: e.scalar_tensor_tensor(
                                out=cc_[:, 1:Tt], in0=psum[bk][:, 0:Tt - 1], scalar=w1, in1=cc_[:, 1:Tt],
                                op0=ALU.mult, op1=ALU.add), reads=[B_ps[bk], bc, B_consts], writes=[bc]))
                            Dv.append(lambda cc_=cc_, bk=bk, w0=w0, bc=bc: P.emit("dve", lambda e: e.scalar_tensor_tensor(
                                out=cc_[:, 2:Tt], in0=psum[bk][:, 0:Tt - 2], scalar=w0, in1=cc_[:, 2:Tt],
                                op0=ALU.mult, op1=ALU.add), reads=[B_ps[bk], bc, B_consts], writes=[bc]))
                            Dv.append(lambda cc_=cc_, hold=hold, w0=w0, bc=bc, rh=rh: P.emit("dve", lambda e: e.scalar_tensor_tensor(
                                out=cc_[:, 0:2], in0=hold, scalar=w0, in1=cc_[:, 0:2],
                                op0=ALU.mult, op1=ALU.add), reads=[rh, bc, B_consts], writes=[bc]))
                            Dv.append(lambda cc_=cc_, hold=hold, w1=w1, bc=bc, rh=rh: P.emit("dve", lambda e: e.scalar_tensor_tensor(
                                out=cc_[:, 0:1], in0=hold[:, 1:2], scalar=w1, in1=cc_[:, 0:1],
                                op0=ALU.mult, op1=ALU.add), reads=[rh, bc, B_consts], writes=[bc]))
                    for f in acts[0] + acts[1]:
                        f()
                    g_, u_ = dves
                    order = []
                    for k in range(max(len(g_), len(u_))):
                        if k < len(g_):
                            order.append(g_[k])
                        if k < len(u_):
                            order.append(u_[k])
                    for f in order:
                        f()
                    if not is_meta:
                        P.emit("act", lambda e, c0=cb_[0], sgb=sgb: e.activation(out=sgb[:, 0:Tt], in_=c0[:, 0:Tt], func=AF.Silu),
                               reads=[bcb[0]], writes=[bsgb])
                        P.emit("dve", lambda e, c1=cb_[1], jc=jc, sgb=sgb: e.tensor_tensor(out=arena[:, jc * T:(jc + 1) * T], in0=sgb[:, 0:Tt], in1=c1[:, 0:Tt], op=ALU.mult),
                               reads=[bsgb, bcb[1]], writes=[B_ar[jc]])
                    dbl ^= 1
                S.release(slu)

        def phase_wdown(j):
            for u in range(6):
                wd, bwd, sld = S.acquire("w", U_DN + u)
                for ki in range(4):
                    kc = 4 * u + ki
                    if kc >= NFC:
                        break
                    for s in range(4):
                        for half in range(2):
                            bk = s * 2 + half
                            mm(psum[bk][:, :], arena[:, kc * T + s * 128:kc * T + (s + 1) * 128], wd[:, ki * 1024 + half * 512:ki * 1024 + (half + 1) * 512],
                               kc == 0, kc == NFC - 1, [bwd, B_ar[kc]], [B_ps[bk]],
                               kc == NFC - 1 or (ki == 3 and bk == 7))
                S.release(sld)
            for s in range(4):
                for half in range(2):
                    bk = s * 2 + half
                    P.emit("dve", lambda e, s=s, half=half, bk=bk: e.scalar_tensor_tensor(
                        out=htm[s][:, half * 512:(half + 1) * 512], in0=htm[s][:, half * 512:(half + 1) * 512], scalar=ALPHA,
                        in1=psum[bk][:, :], op0=ALU.mult, op1=ALU.add),
                        reads=[B_htm[s], B_ps[bk]], writes=[B_htm[s]])

        def phase_ln2_out(j, subs=(0, 1, 2, 3)):
            for s in subs:
                layer_norm(htm[s], B_htm[s], 128, 4, "pool")
                ev = P.dma("sp", lambda e, s=s: e.dma_start(out=out_d[j * T + s * 128:j * T + (s + 1) * 128, :], in_=htm[s][:, :]),
                           f"out{s}", reads=[B_htm[s]])
                out_events.append(ev)

        def aff_all(buf, bbuf, NS, R, gi, extra=()):
            for s in range(NS):
                ln_affine(buf[s], bbuf[s], R, gi, "pool", extra)

        phase_load_ln_in(-1)
        transpose_to_hT(xn, B_xn, 1, NMETA, NMETA, 0)
        aff_all(xn, B_xn, 1, NMETA, 0)
        phase_front(-1)
        phase_attn(-1, lambda h: phase_pool(-1) if h == 6 else None)
        phase_merge_wout_ln1(-1)
        transpose_to_hT(htm, B_htm, 1, NMETA, NMETA, 2, to_meta=True)
        phase_load_ln_in(0)
        transpose_to_hT(xn, B_xn, 4, 128, T, 0)
        aff_all(xn, B_xn, 4, 128, 0)
        for j in range(NT):
            phase_front(j)

            def hook(h, j=j):
                if j > 0 and 2 <= h <= 5:
                    phase_ln2_out(j - 1, (h - 2,))
                if h == 6:
                    phase_pool(j)
            phase_attn(j, hook)
            phase_merge_wout_ln1(j)
            transpose_to_hT(htm, B_htm, 4, 128, T, 2)
            if j + 1 < NT:
                phase_load_ln_in(j + 1)
            phase_ffn_up(j)
            aff_all(htm, B_htm, 4, 128, 2, extra=[B_ar[NFC - 1]])
            if j + 1 < NT:
                transpose_to_hT(xn, B_xn, 4, 128, T, 0)
                aff_all(xn, B_xn, 4, 128, 0)
            phase_wdown(j)
        phase_ln2_out(NT - 1)
        last = {}
        for src, val in out_events:
            last[src] = max(last.get(src, 0), val)
        P.wait_all("sp", list(last.items()))
        P.build(nc, st)
    print(f"[kernel] NT={NT} ops={P.nops} waits={P.nwaits} per-engine={ {e: len(P.ops[e]) for e in P.ENG} }", flush=True)
    return nc


def _unit(arr2d, kc, ncols_pad=None):
    n = arr2d.shape[1]
    a = arr2d.reshape(kc, 128, n).transpose(1, 0, 2).reshape(128, kc * n)
    out = np.zeros((128, USZ), np.float32)
    out[:, :kc * n] = a
    return out


def prep_shared(inp, LTOT):
    f = np.float32
    w_in = np.asarray(inp["w_in"][0], f)
    units = np.zeros((NUNITS, 128, USZ), f)
    units[U_INA] = _unit(w_in[:, 0:512], 8)
    B = np.zeros((1024, 512), f)
    B[:, 0:384] = w_in[:, 512:896]
    B[:, 384:416] = w_in[:, 1152:1184]
    B[:, 416:432] = w_in[:, 1168:1184]
    B[:, 432:448] = w_in[:, 1152:1168]
    units[U_INB] = _unit(B, 8)
    units[U_INC] = _unit(w_in[:, 896:1152], 8)
    units[U_PP] = _unit(np.asarray(inp["p_pool"][0], f), 4)
    wuq = np.asarray(inp["w_uq"][0], f)
    Q = np.zeros((384, 1152), f)
    Q[:, 0:768] = wuq.reshape(384, 768)
    for h in range(8):
        g, i = h // 3, h % 3
        c0 = 768 + g * 128 + i * 32
        Q[:, c0:c0 + 16] = wuq[:, h, 80:96]
        Q[:, c0 + 16:c0 + 32] = wuq[:, h, 64:80]
    units[U_UQ] = _unit(Q, 3)
    KV = np.concatenate([np.asarray(inp["w_uk"][0], f).reshape(256, 512), np.asarray(inp["w_uv"][0], f).reshape(256, 512)], axis=1)
    units[U_UKV] = _unit(KV, 2)
    units[U_PMLA] = _unit(np.asarray(inp["p_mla"][0], f), 4)
    for u in range(4):
        G = np.zeros((1024, 512), f)
        for mi in range(2):
            m = 2 * u + mi
            G[:, mi * 256:mi * 256 + 128] = w_in[:, 1184 + m * 128:1184 + (m + 1) * 128]
            G[:, mi * 256 + 128:mi * 256 + 256] = w_in[:, 1184 + 1024 + m * 128:1184 + 1024 + (m + 1) * 128]
        units[U_G0 + u] = _unit(G, 8)
    w_out = np.asarray(inp["w_out"][0], f)
    for half in range(2):
        units[U_WO + half] = _unit(w_out[:, half * 512:(half + 1) * 512], 8)
    w_up = np.asarray(inp["w_ffn_up"][0], f)
    for u in range(11):
        Ub = np.zeros((1024, 512), f)
        for mi in range(2):
            jc = 2 * u + mi
            Ub[:, mi * 128:(mi + 1) * 128] = w_up[:, jc * 128:(jc + 1) * 128]
            Ub[:, 256 + mi * 128:256 + (mi + 1) * 128] = w_up[:, DFF + jc * 128:DFF + (jc + 1) * 128]
        units[U_UP + u] = _unit(Ub, 8)
    w_dn = np.asarray(inp["w_ffn_down"][0], f)
    for u in range(6):
        k0, k1 = 4 * u, min(4 * u + 4, NFC)
        Dn = np.zeros((512, 1024), f)
        Dn[:(k1 - k0) * 128] = w_dn[k0 * 128:k1 * 128]
        units[U_DN + u] = _unit(Dn, 4)
    poolw = np.ascontiguousarray(np.asarray(inp["pool_w"][0], f).transpose(1, 0, 2).reshape(128, 512))
    lnbc = np.stack([np.broadcast_to(np.asarray(a, f).reshape(1, D), (128, D)) for a in
                     (inp["ln_in_g"], inp["ln_in_b"], inp["ln1_g"][0], inp["ln1_b"][0], inp["ln2_g"][0], inp["ln2_b"][0])]).copy()
    consts = np.zeros((128, NCONST), f)
    consts[:, C_BG:C_BG + 16] = np.asarray(inp["b_gate"][0], f).reshape(16, 128).T
    cw = np.asarray(inp["ffn_conv_w"][0], f)
    for k in range(3):
        consts[:, C_CW + k * 44:C_CW + (k + 1) * 44] = cw[k].reshape(44, 128).T
    consts[:, C_CB:C_CB + 44] = np.asarray(inp["ffn_conv_b"][0], f).reshape(44, 128).T
    consts[:, C_PS:C_PS + 4] = np.asarray(inp["pool_scale"][0], f).reshape(4, 128).T
    consts[:, C_QG:C_QG + 3] = np.asarray(inp["q_norm_g"][0], f).reshape(3, 128).T
    consts[:, C_KG:C_KG + 2] = np.asarray(inp["kv_norm_g"][0], f).reshape(2, 128).T
    for g, wv in enumerate((2, 4, 8, 16)):
        consts[:, C_IC + g * 16:C_IC + (g + 1) * 16] = (1.0 / np.minimum(np.arange(16) + 1, wv)).astype(f)[None, :]
    consts[:, C_EPS] = EPS
    for wi, a in enumerate((inp["ln_in_g"], inp["ln_in_b"], inp["ln1_g"][0], inp["ln1_b"][0])):
        consts[:, C_LT + wi * 8:C_LT + (wi + 1) * 8] = np.asarray(a, f).reshape(8, 128).T
    pos = np.arange(LTOT, dtype=f)
    inv = (f(10000.0) ** (-np.arange(0, 32, 2, dtype=f) / f(32))).astype(f)
    ang = (pos[None, :] * inv[:, None]).astype(f)
    cosv, sinv = np.cos(ang).astype(f), np.sin(ang).astype(f)
    rope = np.stack([np.concatenate([cosv, cosv], 0), np.concatenate([-sinv, sinv], 0)]).astype(f)
    return dict(meta=np.ascontiguousarray(np.asarray(inp["meta"], f)), lnbc=lnbc, consts=consts,
                ident=np.eye(128, dtype=f), rope=np.ascontiguousarray(rope), poolw=poolw, wts=units)


_CACHE = {}


def kernel(**inputs):
    x = np.asarray(inputs["x"], np.float32)
    Bn, Sq, _ = x.shape
    NT = Sq // T
    shared = prep_shared(inputs, NMETA + NT * T)
    if NT not in _CACHE:
        _CACHE[NT] = build_program(NT)
    nc = _CACHE[NT]
    in_maps = [dict(shared, x=np.ascontiguousarray(x[b])) for b in range(Bn)]
    res = run_bass_kernel_spmd(nc, in_maps, core_ids=list(range(Bn)))
    return np.stack([np.asarray(r["out"], np.float32) for r in res.results], axis=0)
```
